# Optimizing a Trainium2 kernel written in Bass

```python
import math
import jax
import jax.numpy as jnp
from jax import lax
import numpy as np

D_MODEL = 1024
BATCH = 1
SEQ = 16384
DEPTH = 2
DEC_BATCH = 32
DEC_SEQ = 1
PAST_LEN = 16384
PAGE_SIZE = 128

N_A_LAYERS = DEPTH // 2
N_B_LAYERS = DEPTH - N_A_LAYERS

A_HEAD_DIM = 128
A_HEADS = D_MODEL // A_HEAD_DIM
A_WIDTH = A_HEADS * A_HEAD_DIM
A_CONV = 4
CHUNK = 64
A_PROJ = 3 * A_WIDTH + 2 * A_HEADS + A_WIDTH

HEAD_DIM = 128
GROUPS = ((128, 1), (512, 4), (2048, 16))
N_GROUPS = len(GROUPS)
Q_PER_GROUP = 4
KV_PER_GROUP = 2
Q_REP = Q_PER_GROUP // KV_PER_GROUP
N_Q_HEADS = N_GROUPS * Q_PER_GROUP
N_KV_HEADS = N_GROUPS * KV_PER_GROUP
ROPE_DIM = HEAD_DIM // 4
ROPE_THETA = 500000.0

D_FF = 2816
FFN_CONV = 3
PLE_DIM = 256
EPS = 1e-6
F32 = jnp.float32

kernel_name = 'yoco_gated_deltanet_dilated_window_step'


def rmsnorm(x, gain):
    xf = x.astype(F32)
    y = xf * lax.rsqrt(jnp.mean(xf * xf, axis=-1, keepdims=True) + EPS)
    return (y * gain.astype(F32)).astype(x.dtype)


def l2norm(x):
    return x * lax.rsqrt(jnp.sum(x * x, axis=-1, keepdims=True) + EPS)


def partial_rope(x, pos):
    half = ROPE_DIM // 2
    inv_freq = ROPE_THETA ** (-jnp.arange(half, dtype=F32) * 2.0 / ROPE_DIM)
    ang = pos.astype(F32)[:, None] * inv_freq[None, :]
    shape = (1, pos.shape[0]) + (1,) * (x.ndim - 3) + (half,)
    cos = jnp.cos(ang).reshape(shape)
    sin = jnp.sin(ang).reshape(shape)
    xr = x[..., :ROPE_DIM].astype(F32)
    x1, x2 = xr[..., :half], xr[..., half:]
    rot = jnp.concatenate([x1 * cos - x2 * sin, x2 * cos + x1 * sin], axis=-1).astype(x.dtype)
    return jnp.concatenate([rot, x[..., ROPE_DIM:]], axis=-1)


def causal_dwconv(x, buf, w):
    width, t = w.shape[0], x.shape[1]
    xp = jnp.concatenate([buf.astype(x.dtype), x], axis=1)
    y = xp[:, 0:t] * w[0]
    for j in range(1, width):
        y = y + xp[:, j:j + t] * w[j]
    return y, xp[:, t:]


def gated_delta_chunked(q, k, v, beta, g, s0):
    b, t, h, _ = q.shape
    dv = v.shape[-1]
    nc = t // CHUNK

    def to_chunks(a):
        a = a.reshape((b, nc, CHUNK, h) + a.shape[3:])
        return jnp.moveaxis(a, 3, 2)

    q, k, v, beta, g = (to_chunks(a) for a in (q, k, v, beta, g))
    gc = jnp.cumsum(g, axis=-1)
    idx = jnp.arange(CHUNK)
    incl = idx[:, None] >= idx[None, :]
    strict = idx[:, None] > idx[None, :]
    decay = jnp.exp(jnp.where(incl, gc[..., :, None] - gc[..., None, :], -jnp.inf))
    kk = jnp.einsum('bnhic,bnhjc->bnhij', k, k)
    tri = jnp.eye(CHUNK, dtype=F32) + jnp.where(strict, beta[..., :, None] * kk * decay, 0.0)
    gam = jnp.exp(gc)
    rhs = jnp.concatenate([v * beta[..., None], k * (beta * gam)[..., None]], axis=-1)
    sol = lax.linalg.triangular_solve(tri, rhs, left_side=True, lower=True, unit_diagonal=True)
    u0, wk = sol[..., :dv], sol[..., dv:]
    qk = jnp.einsum('bnhic,bnhjc->bnhij', q, k) * decay
    qg = q * gam[..., None]
    kd = k * jnp.exp(gc[..., -1:] - gc)[..., None]
    gend = jnp.exp(gc[..., -1])

    def step(s, xs):
        u0c, wkc, qkc, qgc, kdc, gec = xs
        u = u0c - jnp.einsum('bhck,bhkv->bhcv', wkc, s)
        o = jnp.einsum('bhck,bhkv->bhcv', qgc, s) + jnp.einsum('bhij,bhjv->bhiv', qkc, u)
        s = gec[..., None, None] * s + jnp.einsum('bhck,bhcv->bhkv', kdc, u)
        return s, o

    xs = tuple(jnp.moveaxis(a, 1, 0) for a in (u0, wk, qk, qg, kd, gend))
    s1, o = lax.scan(step, s0, xs)
    o = jnp.moveaxis(jnp.moveaxis(o, 0, 1), 2, 3).reshape(b, t, h, dv)
    return o, s1


def gated_delta_steps(q, k, v, beta, g, s0):
    def step(s, xs):
        qt, kt, vt, bt, gt = xs
        s = jnp.exp(gt)[..., None, None] * s
        u = bt[..., None] * (vt - jnp.einsum('bhk,bhkv->bhv', kt, s))
        s = s + kt[..., :, None] * u[..., None, :]
        return s, jnp.einsum('bhk,bhkv->bhv', qt, s)

    xs = tuple(jnp.moveaxis(a, 1, 0) for a in (q, k, v, beta, g))
    s1, o = lax.scan(step, s0, xs)
    return jnp.moveaxis(o, 0, 1), s1


def deltanet_mixer(hn, qkv_buf, s0, w_in, w_conv, a_log, dt_bias, g_out, w_out, chunked):
    b, t, _ = hn.shape
    proj = hn @ w_in
    o1, o2 = 3 * A_WIDTH, 3 * A_WIDTH + A_HEADS
    qkv, qkv_buf = causal_dwconv(proj[..., :o1], qkv_buf, w_conv)
    qkv = jax.nn.silu(qkv.astype(F32)).reshape(b, t, 3, A_HEADS, A_HEAD_DIM)
    q = l2norm(qkv[:, :, 0]) * (A_HEAD_DIM ** -0.5)
    k = l2norm(qkv[:, :, 1])
    v = qkv[:, :, 2]
    a = proj[..., o1:o2].astype(F32)
    beta = jax.nn.sigmoid(proj[..., o2:o2 + A_HEADS].astype(F32))
    z = proj[..., o2 + A_HEADS:].reshape(b, t, A_HEADS, A_HEAD_DIM).astype(F32)
    g = -jnp.exp(a_log.astype(F32)) * jax.nn.softplus(a + dt_bias.astype(F32))
    s0 = s0.astype(F32)
    if chunked:
        o, s1 = gated_delta_chunked(q, k, v, beta, g, s0)
    else:
        o, s1 = gated_delta_steps(q, k, v, beta, g, s0)
    o = rmsnorm(o, g_out) * jax.nn.silu(z)
    return o.reshape(b, t, A_WIDTH).astype(hn.dtype) @ w_out, s1, qkv_buf


def conv_ffn(hn, buf, w_up, w_conv, w_down):
    u, buf = causal_dwconv(hn @ w_up, buf, w_conv)
    return (jax.nn.silu(u[..., :D_FF]) * u[..., D_FF:]) @ w_down, buf


def shared_kv(h, pos, g_kv_norm, w_kv, g_k_norm):
    b, t, _ = h.shape
    kv = (rmsnorm(h, g_kv_norm) @ w_kv).reshape(b, t, 2, N_KV_HEADS, HEAD_DIM)
    k = partial_rope(rmsnorm(kv[:, :, 0], g_k_norm), pos)
    return k, kv[:, :, 1]


def dilated_band_prompt(q, k, v, window, dil):
    b, s = q.shape[:2]
    ln = s // dil
    n = window // dil
    nb = -(-ln // n)
    lp = nb * n

    def strided(a):
        a = jnp.moveaxis(a.reshape((b, ln, dil) + a.shape[2:]), 2, 1)
        a = jnp.pad(a, [(0, 0), (0, 0), (0, lp - ln)] + [(0, 0)] * (a.ndim - 3))
        return a.reshape((b, dil, nb, n) + a.shape[3:])

    def with_prev(a):
        prev = jnp.pad(a, [(0, 0), (0, 0), (1, 0)] + [(0, 0)] * (a.ndim - 3))[:, :, :-1]
        return jnp.concatenate([prev, a], axis=3)

    qb = strided(q)
    kw, vw = with_prev(strided(k)), with_prev(strided(v))
    sc = jnp.einsum('brnigec,brnjgc->brngeij', qb, kw, preferred_element_type=F32) * (HEAD_DIM ** -0.5)
    ii = jnp.arange(n)[:, None]
    jj = jnp.arange(2 * n)[None, :]
    dist = n + ii - jj
    band = (dist >= 0) & (dist <= n)
    first = (jnp.arange(nb)[:, None, None] > 0) | (jj >= n)[None]
    mask = band[None] & first
    sc = jnp.where(mask[None, None, :, None, None], sc, -jnp.inf)
    m = jnp.max(sc, axis=-1, keepdims=True)
    pe = jnp.exp(sc - m)
    den = jnp.sum(pe, axis=-1, keepdims=True)
    o = jnp.einsum('brngeij,brnjgc->brnigec', pe, vw)
    o = o / jnp.moveaxis(den[..., 0], -1, 3)[..., None]
    lse = jnp.moveaxis((m + jnp.log(den))[..., 0], -1, 3)

    def unstride(a):
        rest = a.shape[4:]
        a = a.reshape((b, dil, lp) + rest)[:, :, :ln]
        return jnp.moveaxis(a, 1, 2).reshape((b, s) + rest)

    return unstride(o), unstride(lse)


def dilated_gather_sample(q, k, v, cache, window, dil):
    t = q.shape[1]
    wb = cache.shape[1]
    keys = jnp.concatenate([cache[:, :, 0].astype(k.dtype), k], axis=1)
    vals = jnp.concatenate([cache[:, :, 1].astype(v.dtype), v], axis=1)
    n_keys = window // dil + 1
    idx = wb + jnp.arange(t)[:, None] - dil * jnp.arange(n_keys)[None, :]
    valid = idx >= 0
    idx = jnp.maximum(idx, 0)
    kg, vg = keys[:, idx], vals[:, idx]
    sc = jnp.einsum('btgec,btmgc->btgem', q, kg, preferred_element_type=F32) * (HEAD_DIM ** -0.5)
    sc = jnp.where(valid[None, :, None, None, :], sc, -jnp.inf)
    m = jnp.max(sc, axis=-1, keepdims=True)
    pe = jnp.exp(sc - m)
    den = jnp.sum(pe, axis=-1, keepdims=True)
    o = jnp.einsum('btgem,btmgc->btgec', pe, vg) / den
    return o, (m + jnp.log(den))[..., 0]


def dilated_mixer(hn, k_sh, v_sh, pos, w_q, g_q_norm, w_o, caches):
    b, t, _ = hn.shape
    q = (hn @ w_q).reshape(b, t, N_GROUPS, KV_PER_GROUP, Q_REP, HEAD_DIM)
    q = partial_rope(rmsnorm(q, g_q_norm), pos)
    outs, lses = [], []
    for gi, (window, dil) in enumerate(GROUPS):
        sl = slice(gi * KV_PER_GROUP, (gi + 1) * KV_PER_GROUP)
        if caches is None:
            o, lse = dilated_band_prompt(q[:, :, gi], k_sh[:, :, sl], v_sh[:, :, sl], window, dil)
        else:
            o, lse = dilated_gather_sample(q[:, :, gi], k_sh[:, :, sl], v_sh[:, :, sl], caches[gi], window, dil)
        outs.append(o)
        lses.append(lse)
    wts = jax.nn.softmax(jnp.stack(lses, axis=2), axis=2)
    o = jnp.sum(wts[..., None] * jnp.stack(outs, axis=2), axis=2)
    return o.reshape(b, t, Q_PER_GROUP * HEAD_DIM).astype(hn.dtype) @ w_o


def run_trunk(x, p, pos, state, prm):
    is_prompt = state is None
    b, t, _ = x.shape
    dt = x.dtype
    h = x
    new_delta, new_qkv, new_ffn = [], [], []
    k_sh = v_sh = None
    new_kv = None
    for i in range(DEPTH):
        hn = rmsnorm(h, prm['g_mix_norm'][i])
        if i < N_A_LAYERS:
            if is_prompt:
                qkv_buf = jnp.zeros((b, A_CONV - 1, 3 * A_WIDTH), dt)
                s0 = jnp.zeros((b, A_HEADS, A_HEAD_DIM, A_HEAD_DIM), F32)
            else:
                s0, qkv_buf = state[0][i], state[1][i]
            mix, s1, qkv_buf1 = deltanet_mixer(hn, qkv_buf, s0, prm['w_a_in'][i], prm['w_a_conv'][i],
                                               prm['a_log'][i], prm['a_dt_bias'][i], prm['g_a_out_norm'][i],
                                               prm['w_a_out'][i], is_prompt)
            new_delta.append(s1.astype(dt))
            new_qkv.append(qkv_buf1)
        else:
            j = i - N_A_LAYERS
            mix = dilated_mixer(hn, k_sh, v_sh, pos, prm['w_q'][j], prm['g_q_norm'][j], prm['w_o'][j],
                                None if is_prompt else state[3])
        h = h + mix
        ffn_buf = jnp.zeros((b, FFN_CONV - 1, 2 * D_FF), dt) if is_prompt else state[2][i]
        f, ffn_buf1 = conv_ffn(rmsnorm(h, prm['g_ffn_norm'][i]), ffn_buf, prm['w_ffn_up'][i],
                               prm['w_ffn_conv'][i], prm['w_ffn_down'][i])
        h = h + f
        gate = jax.nn.sigmoid(rmsnorm(h, prm['g_ple_norm'][i]) @ prm['w_ple_gate'][i])
        h = h + gate * (p[i] @ prm['w_ple_proj'][i])
        new_ffn.append(ffn_buf1)
        if i == N_A_LAYERS - 1:
            k_sh, v_sh = shared_kv(h, pos, prm['g_kv_norm'], prm['w_kv'], prm['g_k_norm'])
            kv_rows = jnp.stack([k_sh, v_sh], axis=2)
            new_kv = tuple(
                kv_rows[:, t - (min(w, t) if is_prompt else t):, :, gi * KV_PER_GROUP:(gi + 1) * KV_PER_GROUP]
                for gi, (w, _) in enumerate(GROUPS))
    return h, jnp.stack(new_delta), jnp.stack(new_qkv), jnp.stack(new_ffn), new_kv


def setup_inputs(seed: int = 0) -> dict:
    key = jax.random.key(seed)
    ks = iter(jax.random.split(key, 32))

    def nrm(shape, scale):
        return jax.random.normal(next(ks), shape, F32) * scale

    def gain(shape):
        return 1.0 + 0.02 * jax.random.normal(next(ks), shape, F32)

    x_prompt = nrm((BATCH, SEQ, D_MODEL), 1.0)
    x_sample = nrm((DEC_BATCH, DEC_SEQ, D_MODEL), 1.0)
    p_prompt = nrm((DEPTH, BATCH, SEQ, PLE_DIM), 1.0)
    p_sample = nrm((DEPTH, DEC_BATCH, DEC_SEQ, PLE_DIM), 1.0)
    state_delta = nrm((N_A_LAYERS, DEC_BATCH, A_HEADS, A_HEAD_DIM, A_HEAD_DIM), 0.05)
    state_qkv_conv = nrm((N_A_LAYERS, DEC_BATCH, A_CONV - 1, 3 * A_WIDTH), 1.0)
    state_ffn_conv = nrm((DEPTH, DEC_BATCH, FFN_CONV - 1, 2 * D_FF), 1.0)
    cache_kv_w128 = nrm((DEC_BATCH, min(GROUPS[0][0], PAST_LEN), 2, KV_PER_GROUP, HEAD_DIM), 1.0)
    cache_kv_w512 = nrm((DEC_BATCH, min(GROUPS[1][0], PAST_LEN), 2, KV_PER_GROUP, HEAD_DIM), 1.0)
    cache_kv_w2048 = nrm((DEC_BATCH, min(GROUPS[2][0], PAST_LEN), 2, KV_PER_GROUP, HEAD_DIM), 1.0)
    g_mix_norm = gain((DEPTH, D_MODEL))
    g_ffn_norm = gain((DEPTH, D_MODEL))
    w_ffn_up = nrm((DEPTH, D_MODEL, 2 * D_FF), D_MODEL ** -0.5)
    w_ffn_conv = nrm((DEPTH, FFN_CONV, 2 * D_FF), FFN_CONV ** -0.5)
    w_ffn_down = nrm((DEPTH, D_FF, D_MODEL), D_FF ** -0.5)
    g_ple_norm = gain((DEPTH, D_MODEL))
    w_ple_gate = nrm((DEPTH, D_MODEL, D_MODEL), D_MODEL ** -0.5)
    w_ple_proj = nrm((DEPTH, PLE_DIM, D_MODEL), PLE_DIM ** -0.5)
    w_a_in = nrm((N_A_LAYERS, D_MODEL, A_PROJ), D_MODEL ** -0.5)
    w_a_conv = nrm((N_A_LAYERS, A_CONV, 3 * A_WIDTH), A_CONV ** -0.5)
    a_log = jnp.log(jax.random.uniform(next(ks), (N_A_LAYERS, A_HEADS), F32, 1.0, 16.0))
    dt = jnp.exp(jax.random.uniform(next(ks), (N_A_LAYERS, A_HEADS), F32, math.log(1e-3), math.log(1e-1)))
    a_dt_bias = dt + jnp.log(-jnp.expm1(-dt))
    g_a_out_norm = gain((N_A_LAYERS, A_HEAD_DIM))
    w_a_out = nrm((N_A_LAYERS, A_WIDTH, D_MODEL), A_WIDTH ** -0.5)
    g_kv_norm = gain((D_MODEL,))
    w_kv = nrm((D_MODEL, 2 * N_KV_HEADS * HEAD_DIM), D_MODEL ** -0.5)
    g_k_norm = gain((HEAD_DIM,))
    w_q = nrm((N_B_LAYERS, D_MODEL, N_Q_HEADS * HEAD_DIM), D_MODEL ** -0.5)
    g_q_norm = gain((N_B_LAYERS, HEAD_DIM))
    w_o = nrm((N_B_LAYERS, Q_PER_GROUP * HEAD_DIM, D_MODEL), (Q_PER_GROUP * HEAD_DIM) ** -0.5)
    return {'x_prompt': x_prompt, 'x_sample': x_sample, 'p_prompt': p_prompt, 'p_sample': p_sample,
            'state_delta': state_delta, 'state_qkv_conv': state_qkv_conv, 'state_ffn_conv': state_ffn_conv,
            'cache_kv_w128': cache_kv_w128, 'cache_kv_w512': cache_kv_w512, 'cache_kv_w2048': cache_kv_w2048,
            'g_mix_norm': g_mix_norm, 'g_ffn_norm': g_ffn_norm, 'w_ffn_up': w_ffn_up, 'w_ffn_conv': w_ffn_conv,
            'w_ffn_down': w_ffn_down, 'g_ple_norm': g_ple_norm, 'w_ple_gate': w_ple_gate, 'w_ple_proj': w_ple_proj,
            'w_a_in': w_a_in, 'w_a_conv': w_a_conv, 'a_log': a_log, 'a_dt_bias': a_dt_bias,
            'g_a_out_norm': g_a_out_norm, 'w_a_out': w_a_out, 'g_kv_norm': g_kv_norm, 'w_kv': w_kv,
            'g_k_norm': g_k_norm, 'w_q': w_q, 'g_q_norm': g_q_norm, 'w_o': w_o}


def reference(x_prompt, x_sample, p_prompt, p_sample, state_delta, state_qkv_conv, state_ffn_conv,
              cache_kv_w128, cache_kv_w512, cache_kv_w2048, g_mix_norm, g_ffn_norm, w_ffn_up, w_ffn_conv,
              w_ffn_down, g_ple_norm, w_ple_gate, w_ple_proj, w_a_in, w_a_conv, a_log, a_dt_bias,
              g_a_out_norm, w_a_out, g_kv_norm, w_kv, g_k_norm, w_q, g_q_norm, w_o):
    prm = dict(g_mix_norm=g_mix_norm, g_ffn_norm=g_ffn_norm, w_ffn_up=w_ffn_up, w_ffn_conv=w_ffn_conv,
               w_ffn_down=w_ffn_down, g_ple_norm=g_ple_norm, w_ple_gate=w_ple_gate, w_ple_proj=w_ple_proj,
               w_a_in=w_a_in, w_a_conv=w_a_conv, a_log=a_log, a_dt_bias=a_dt_bias,
               g_a_out_norm=g_a_out_norm, w_a_out=w_a_out, g_kv_norm=g_kv_norm, w_kv=w_kv,
               g_k_norm=g_k_norm, w_q=w_q, g_q_norm=g_q_norm, w_o=w_o)
    pos_prompt = jnp.arange(x_prompt.shape[1], dtype=jnp.int32)
    pos_sample = PAST_LEN + jnp.arange(x_sample.shape[1], dtype=jnp.int32)
    y_prompt, sd_p, sq_p, sf_p, kv_p = run_trunk(x_prompt, p_prompt, pos_prompt, None, prm)
    y_sample, sd_s, sq_s, sf_s, kv_s = run_trunk(
        x_sample, p_sample, pos_sample,
        (state_delta, state_qkv_conv, state_ffn_conv, (cache_kv_w128, cache_kv_w512, cache_kv_w2048)), prm)
    kv128_p, kv512_p, kv2048_p = kv_p
    kv128_s, kv512_s, kv2048_s = kv_s
    return (y_prompt, y_sample, sd_p, sq_p, sf_p, kv128_p, kv512_p, kv2048_p,
            sd_s, sq_s, sf_s, kv128_s, kv512_s, kv2048_s)
```

```python
import numpy as np
import ml_dtypes
from contextlib import ExitStack
import concourse.bass as bass
import concourse.mybir as mybir
from concourse.bass_utils import run_bass_kernel_spmd

F32 = mybir.dt.float32
BF16 = mybir.dt.bfloat16
AF = mybir.ActivationFunctionType
ALU = mybir.AluOpType

NCORES = 8
T_P = 16384
NS = 32
D = 1024
HD = 128
EPS = 1e-6
NDS = 40


class Buf:
    __slots__ = ("name", "w", "r", "excl")

    def __init__(self, name=""):
        self.name = name
        self.w = None
        self.r = {}
        self.excl = False


class T:
    def __init__(self, ap, buf=None):
        self.ap = ap
        self.buf = buf if buf is not None else Buf()

    def __getitem__(self, idx):
        return T(self.ap[idx], self.buf)

    def re(self, pat, **kw):
        return T(self.ap.rearrange(pat, **kw), self.buf)

    def sub(self, idx):
        return T(self.ap[idx], Buf())


def A(x):
    return x.ap if isinstance(x, T) else x


def B(*xs):
    return [x.buf for x in xs if isinstance(x, T)]


class Sched:
    ENG = ("pe", "act", "dve", "pool", "sp")

    def __init__(self, nc, es):
        self.nc = nc
        self.hd = {}
        self.cnt = {}
        self.known = {}
        self.ops = {}
        for e in self.ENG:
            self.hd[e] = es.enter_context(nc.semaphore("s_" + e))
            self.cnt[e] = 0
            self.known[e] = {}
            self.ops[e] = []
        for i in range(NDS):
            self.hd[("d", i)] = es.enter_context(nc.semaphore("d%d" % i))
        self.hd["cc"] = es.enter_context(nc.semaphore("cc"))
        self.tiny = T(es.enter_context(nc.sbuf_tensor("tiny", [1, 4], F32))[:])
        self.dval = [0] * NDS
        self.dnext = 0
        self.nops = 0

    def _deps(self, eng, rd, wr):
        need = {}

        def add(k, v):
            if need.get(k, 0) < v:
                need[k] = v

        for b in rd:
            if b.w is not None:
                add(*b.w)
            if b.excl:
                for k, v in b.r.items():
                    if k != eng:
                        add(k, v)
        for b in wr:
            if b.w is not None:
                add(*b.w)
            for k, v in b.r.items():
                add(k, v)
        waits = []
        kn = self.known[eng]
        for k, v in need.items():
            if k == eng and eng == "pe":
                continue
            if kn.get(k, 0) >= v:
                continue
            kn[k] = v
            waits.append((k, v))
        return waits

    def _mark(self, tok, rd, wr):
        k, v = tok
        for b in rd:
            if b.r.get(k, 0) < v:
                b.r[k] = v
        for b in wr:
            b.w = tok
            b.r = {}

    def op(self, eng, fn, rd=(), wr=()):
        waits = self._deps(eng, rd, wr)
        self.cnt[eng] += 1
        tok = (eng, self.cnt[eng])
        self.ops[eng].append((waits, fn, eng, 1))
        self._mark(tok, rd, wr)
        self.nops += 1

    def dma(self, eng, out, in_, **kw):
        rd, wr = B(in_), B(out)
        i = self.dnext
        self.dnext = (self.dnext + 1) % NDS
        waits = self._deps(eng, rd, wr)
        k = ("d", i)
        if self.dval[i] > 0 and self.known[eng].get(k, 0) < self.dval[i]:
            waits.append((k, self.dval[i]))
            self.known[eng][k] = self.dval[i]
        self.dval[i] += 16
        tok = (k, self.dval[i])
        o, s = A(out), A(in_)
        self.ops[eng].append((waits, lambda e: e.dma_start(out=o, in_=s, **kw), k, 16))
        self._mark(tok, rd, wr)
        self.nops += 1

    def flush(self, final=False):
        nc = self.nc
        if final:
            waits = []
            for i in range(NDS):
                if self.dval[i] > 0:
                    waits.append((("d", i), self.dval[i]))
            self.ops["sp"].append((waits, None, None, 0))
        for e in self.ENG:
            waits = []
            kn = self.known[e]
            for k in self.ENG:
                if k != e and kn.get(k, 0) < self.cnt[k]:
                    waits.append((k, self.cnt[k]))
                    kn[k] = self.cnt[k]
            for i in range(NDS):
                k = ("d", i)
                if kn.get(k, 0) < self.dval[i]:
                    waits.append((k, self.dval[i]))
                    kn[k] = self.dval[i]
            if waits:
                self.ops[e].append((waits, None, None, 0))
        hd = self.hd

        def mk(name):
            ops = self.ops[name]

            def body(e):
                self._cur_pid = None
                for waits, fn, k, inc in ops:
                    for wk, wv in waits:
                        e.wait_ge(hd[wk], wv)
                    if fn is not None:
                        fn(e).then_inc(hd[k], inc)

            return body

        with nc.Block() as block:
            block.tensor(mk("pe"))
            block.scalar(mk("act"))
            block.vector(mk("dve"))
            block.gpsimd(mk("pool"))
            block.sync(mk("sp"))
        for e in self.ENG:
            self.ops[e] = []

    def coll_issue(self, ins, outs):
        waits = self._deps("pool", B(ins), B(outs))
        self.ccn = getattr(self, "ccn", 0) + 1
        i_, o_ = A(ins), A(outs)
        self.ops["pool"].append((waits, lambda e: e.collective_compute(
            "AllGather", ALU.bypass, replica_groups=[list(range(NCORES))], ins=[i_], outs=[o_]), "cc", 1))
        return (self.ccn, B(ins), B(outs))

    def coll_relay(self, h):
        n, rd, wr = h
        self.cnt["pool"] += 1
        t = self.tiny.ap
        self.ops["pool"].append(([("cc", n)], lambda e: e.memset(t, 0.0), "pool", 1))
        self._mark(("pool", self.cnt["pool"]), rd, wr)

    def coll(self, ins, outs):
        self.coll_relay(self.coll_issue(ins, outs))

    def _get_pid(self, e):
        if self._cur_pid is None:
            self._cur_pid = e.partition_id()
        return self._cur_pid

    def dma_dyn(self, eng, out, src_fn, rd=()):
        wr = B(out)
        i = self.dnext
        self.dnext = (self.dnext + 1) % NDS
        waits = self._deps(eng, list(rd), wr)
        k = ("d", i)
        if self.dval[i] > 0 and self.known[eng].get(k, 0) < self.dval[i]:
            waits.append((k, self.dval[i]))
            self.known[eng][k] = self.dval[i]
        self.dval[i] += 16
        o = A(out)
        self.ops[eng].append((waits, lambda e: e.dma_start(out=o, in_=src_fn(self._get_pid(e))), k, 16))
        self._mark((k, self.dval[i]), list(rd), wr)
        self.nops += 1

    def mm(self, out, lhsT, rhs, start=True, stop=True):
        o, l, r = A(out), A(lhsT), A(rhs)
        self.op("pe", lambda e: e.matmul(o, lhsT=l, rhs=r, start=start, stop=stop), B(lhsT, rhs), B(out))

    def tr(self, out, in_, ident):
        o, i, d = A(out), A(in_), A(ident)
        self.op("pe", lambda e: e.transpose(o, i, d), B(in_, ident), B(out))

    def act(self, out, in_, func, bias=None, scale=None, accum=None):
        o, i = A(out), A(in_)
        kw = {}
        if bias is not None:
            kw["bias"] = A(bias)
        if scale is not None:
            kw["scale"] = A(scale)
        if accum is not None:
            kw["accum_out"] = A(accum)
        self.op("act", lambda e: e.activation(o, i, func, **kw), B(in_, bias, scale), B(out, accum))

    def ts(self, eng, out, in0, s1, s2, op0, op1=None):
        o, i, a1, a2 = A(out), A(in0), A(s1), A(s2)
        if op1 is None:
            f = lambda e: e.tensor_scalar(o, i, a1, None, op0)
        else:
            f = lambda e: e.tensor_scalar(o, i, a1, a2, op0, op1)
        self.op(eng, f, B(in0, s1, s2), B(out))

    def tt(self, eng, out, in0, in1, op):
        o, i0, i1 = A(out), A(in0), A(in1)
        self.op(eng, lambda e: e.tensor_tensor(o, i0, i1, op), B(in0, in1), B(out))

    def stt(self, eng, out, in0, scalar, in1, op0, op1):
        o, i0, s, i1 = A(out), A(in0), A(scalar), A(in1)
        self.op(eng, lambda e: e.scalar_tensor_tensor(o, i0, s, i1, op0, op1), B(in0, scalar, in1), B(out))

    def cp(self, eng, out, in_):
        o, i = A(out), A(in_)
        if eng == "act":
            self.op("act", lambda e: e.copy(o, i), B(in_), B(out))
        else:
            self.op(eng, lambda e: e.tensor_copy(o, i), B(in_), B(out))

    def recip(self, out, in_):
        o, i = A(out), A(in_)
        self.op("dve", lambda e: e.reciprocal(o, i), B(in_), B(out))

    def memset(self, eng, out, val):
        o = A(out)
        self.op(eng, lambda e: e.memset(o, val), (), B(out))


class Ctx:
    N = [0]

    def __init__(self, nc, es):
        self.nc = nc
        self.es = es

    @property
    def n(self):
        Ctx.N[0] += 1
        return Ctx.N[0]

    def sb(self, shape, dt=F32, name=None):
        t = self.es.enter_context(self.nc.sbuf_tensor("%s_%d" % (name or "sb", self.n), list(shape), dt))
        return T(t[:] if not hasattr(t, "ap") else t.ap())

    def ps(self, shape, dt=F32, name=None):
        t = self.es.enter_context(self.nc.psum_tensor("%s_%d" % (name or "ps", self.n), list(shape), dt))
        r = T(t[:] if not hasattr(t, "ap") else t.ap())
        r.buf.excl = True
        return r

    def dram(self, name, shape, dt, kind):
        return T(self.nc.dram_tensor(name, list(shape), dt, kind=kind).ap())


NBLK_A = T_P // 512
ROWS_A = 4 + T_P + NS


def phase_a(nc, S, io, bo, gath, nblk=NBLK_A, do_samples=True, use_coll=True):
    es = ExitStack()
    with es:
        C = Ctx(nc, es)
        ident = C.sb([128, 128], F32, "ident")
        identb = C.sb([128, 128], BF16, "identb")
        U = C.sb([128, 128], F32, "U")
        Ms = C.sb([128, 128], F32, "Ms")
        ones = C.sb([128, 128], F32, "ones")
        S.dma("sp", ident, io["ident"])
        S.dma("sp", U, io["U"])
        S.dma("sp", Ms, io["Ms"])
        S.memset("dve", ones, 1.0)
        S.cp("dve", identb, ident)
        gmix = C.sb([128, 8], F32, "gmix")
        S.dma("sp", gmix, io["gmix0"])
        wstage = C.sb([128, 8, 514], F32, "wstage")
        S.dma("sp", wstage[:, :, 0:384], io["wqkv"].re("(k p) f -> p k f", p=128))
        S.dma("sp", wstage[:, :, 384:514], io["wzab"].re("(k p) f -> p k f", p=128))
        W = C.sb([128, 8, 514], BF16, "W")
        for k in range(8):
            S.ts("dve", W[:, k, :], wstage[:, k, :], gmix[:, k:k + 1], None, ALU.mult)
        wc = C.sb([128, 3, 4], F32, "wc")
        S.dma("sp", wc, io["wconv"])
        gout = C.sb([128, 128], F32, "gout")
        S.dma("sp", gout, io["gout"])
        alog = C.sb([128, 1], F32, "alog")
        dtb = C.sb([128, 1], F32, "dtb")
        S.dma("sp", alog, io["alog"])
        S.dma("sp", dtb, io["dtb"])
        negA = C.sb([128, 1], F32, "negA")
        S.act(negA, alog, AF.Exp)
        S.ts("dve", negA, negA, -1.0, None, ALU.mult)
        pend = [None]

        xt = [C.sb([128, 1024], F32, "xt") for _ in range(2)]
        junk = C.sb([128, 1024], BF16, "junk")
        ss = [C.sb([128, 2], F32, "ss") for _ in range(2)]
        xn = [C.sb([128, 1024], BF16, "xn") for _ in range(2)]
        xnT = [C.sb([128, 8, 512], BF16, "xnT") for _ in range(2)]
        pre = C.sb([128, 3, 515], F32, "pre")
        S.memset("dve", pre, 0.0)
        acc = C.sb([128, 3, 512], F32, "acc")
        post = [C.sb([128, 3, 512], F32, "post") for _ in range(2)]
        sq = C.sb([128, 2, 512], F32, "sq")
        rinv = C.sb([128, 512], F32, "rinv")
        zab = [[C.sb([128, 130], F32, "zab") for _ in range(4)] for _ in range(2)]
        Sst = [C.sb([128, 128], F32, "Sst") for _ in range(2)]
        S.memset("dve", Sst[0], 0.0)
        tok3 = C.sb([32, 384], F32, "tok3")

        pT = C.ps([128, 1024], BF16, "pT")
        pq = C.ps([128, 512], F32, "pq")
        pz = C.ps([128, 512], F32, "pz")
        po = C.ps([128, 512], F32, "po")
        pn = pq
        pslots = [C.ps([128, 512], F32, "pslot") for _ in range(4)]
        slot_i = [0]

        def slot():
            s = pslots[slot_i[0] % len(pslots)]
            slot_i[0] += 1
            return s[:, 0:128]

        NSET = 2
        cb = []
        for _ in range(NSET):
            d = {}
            for nm in ("gB", "d1", "d2", "E", "EMs", "E2", "ETM", "gamrow", "Kbg", "Vb", "Kd", "L", "M", "L2", "M2",
                       "Y", "QKDt", "QgT", "U0", "WkT", "u", "sz", "gz", "jk", "jk2"):
                d[nm] = C.sb([128, 128], F32, nm)
            d["sc"] = C.sb([128, 12], F32, "sc")
            d["gend"] = C.sb([128, 1], F32, "gend")
            d["og"] = C.sb([128, 128], BF16, "og")
            cb.append(d)
        chunk_no = [0]
        MUL, ADD, SUB = ALU.mult, ALU.add, ALU.subtract

        def chunk(QT, KT, VT, zb, Cn, S_in, S_out, og_dst, nlev):
            b = cb[chunk_no[0] % NSET]
            chunk_no[0] += 1
            c = slice(0, Cn)
            sc = b["sc"]
            g, beta, gam, kds, bg, t0, t1, rso0, rso = (sc[c, i:i + 1] for i in range(9))
            gcs = sc[c, 9:11]
            a_l, b_l, z = zb[c, 128:129], zb[c, 129:130], zb[c, 0:128]
            S.act(t0, a_l, AF.Exp, bias=dtb[c, :])
            S.act(t1, t0, AF.Ln, bias=1.0)
            S.tt("dve", g, t1, negA[c, :], MUL)
            S.act(beta, b_l, AF.Sigmoid)
            pg = slot()
            S.mm(pg[c, 0:1], U[c, c], g)
            S.mm(pg[c, 1:2], ones[c, c], g)
            S.cp("dve", gcs, pg[c, 0:2])
            S.ts("dve", b["gB"][c, :], ones[c, :], g, None, MUL)
            prow = slot()
            S.mm(prow[:, c], b["gB"][c, :], U[c, c])
            S.ts("dve", b["d1"][c, c], prow[c, c], gcs[:, 0:1], 0.0, SUB, ALU.max)
            S.act(b["E"][c, c], b["d1"][c, c], AF.Exp, scale=-1.0)
            S.tt("pool", b["EMs"][c, c], b["E"][c, c], Ms[c, c], MUL)
            S.ts("dve", b["d2"][c, c], prow[c, c], gcs[:, 0:1], 0.0, SUB, ALU.min)
            S.act(b["E2"][c, c], b["d2"][c, c], AF.Exp)
            S.tt("pool", b["ETM"][c, c], b["E2"][c, c], U[c, c], MUL)
            S.act(b["gamrow"][:, c], prow[:, c], AF.Exp)
            S.act(b["gend"], prow[:, Cn - 1:Cn], AF.Exp)
            S.act(gam, gcs[:, 0:1], AF.Exp)
            S.act(kds, gcs[:, 0:1], AF.Exp, bias=gcs[:, 1:2], scale=-1.0)
            S.tt("dve", bg, beta, gam, MUL)
            pt = slot()
            S.tr(pt[c, :], KT, ident)
            pv = slot()
            S.tr(pv[c, :], VT, ident)
            S.ts("dve", b["Kbg"][c, :], pt[c, :], bg, None, MUL)
            S.ts("dve", b["Kd"][c, :], pt[c, :], kds, None, MUL)
            S.ts("dve", b["Vb"][c, :], pv[c, :], beta, None, MUL)
            pk = slot()
            S.mm(pk[c, c], KT, KT)
            S.stt("dve", b["L"][c, c], pk[c, c], beta, b["EMs"][c, c], MUL, MUL)
            pqk = slot()
            S.mm(pqk[c, c], KT, QT)
            S.tt("dve", b["QKDt"][c, c], pqk[c, c], b["ETM"][c, c], MUL)
            S.tt("pool", b["QgT"][:, c], QT, b["gamrow"][:, c], MUL)
            pM = slot()
            S.tr(pM[c, c], b["L"][c, c], ident[c, c])
            S.cp("dve", b["M"][c, c], pM[c, c])
            S.stt("dve", b["Y"][c, c], pM[c, c], -1.0, ident[c, c], MUL, ADD)
            Lp, Mp = b["L"], b["M"]
            alt = [(b["L2"], b["M2"]), (b["L"], b["M"])]
            for lv in range(nlev):
                Ln_, Mn_ = alt[lv % 2]
                pM2 = slot()
                S.mm(pM2[c, c], Lp[c, c], Mp[c, c])
                pL2 = slot()
                S.mm(pL2[c, c], Mp[c, c], Lp[c, c])
                S.cp("dve", Mn_[c, c], pM2[c, c])
                S.cp("dve", Ln_[c, c], pL2[c, c])
                pY = slot()
                S.mm(pY[c, c], Ln_[c, c], b["Y"][c, c])
                S.tt("dve", b["Y"][c, c], pY[c, c], b["Y"][c, c], ADD)
                Lp, Mp = Ln_, Mn_
            pU = slot()
            S.mm(pU[c, :], b["Y"][c, c], b["Vb"][c, :])
            S.cp("dve", b["U0"][c, :], pU[c, :])
            pW = slot()
            S.mm(pW[:, c], b["Kbg"][c, :], b["Y"][c, c])
            S.cp("dve", b["WkT"][:, c], pW[:, c])
            pu = slot()
            S.mm(pu[c, :], b["WkT"][:, c], S_in)
            S.stt("dve", b["u"][c, :], pu[c, :], -1.0, b["U0"][c, :], MUL, ADD)
            S.mm(po[c, 0:128], b["QgT"][:, c], S_in, start=True, stop=False)
            S.mm(po[c, 0:128], b["QKDt"][c, c], b["u"][c, :], start=False, stop=True)
            pS = slot()
            S.mm(pS, b["Kd"][c, :], b["u"][c, :])
            S.ts("pool", b["jk2"], S_in, b["gend"][:, 0:1], None, MUL)
            S.tt("dve", S_out, pS, b["jk2"], ADD)
            S.act(b["jk"][c, :], po[c, 0:128], AF.Square, accum=rso0)
            S.act(rso, rso0, AF.Sqrt, bias=EPS, scale=1.0 / 128)
            S.recip(rso, rso)
            S.act(b["sz"][c, :], z, AF.Silu)
            S.tt("pool", b["gz"][c, :], b["sz"][c, :], gout[c, :], MUL)
            S.stt("dve", b["og"][c, :], po[c, 0:128], rso, b["gz"][c, :], MUL, MUL)
            S.dma("sp", og_dst, b["og"][c, :])

        def conv_norm(prebuf, accb, postb, n):
            for s in range(3):
                S.ts("dve", accb[:, s, 0:n], prebuf[:, s, 0:n], wc[:, s, 0:1], None, MUL)
                for j in range(1, 4):
                    S.stt("dve", accb[:, s, 0:n], prebuf[:, s, j:j + n], wc[:, s, j:j + 1], accb[:, s, 0:n], MUL, ADD)
            for s in range(3):
                S.act(postb[:, s, 0:n], accb[:, s, 0:n], AF.Silu)
            for s in range(2):
                S.tt("pool", sq[:, s, 0:n], postb[:, s, 0:n], postb[:, s, 0:n], MUL)
            for s in range(2):
                S.mm(pn[:, 0:n], ones, sq[:, s, 0:n])
                S.act(rinv[:, 0:n], pn[:, 0:n], AF.Sqrt, bias=EPS)
                S.recip(rinv[:, 0:n], rinv[:, 0:n])
                if s == 0:
                    S.stt("dve", postb[:, 0, 0:n], postb[:, 0, 0:n], HD ** -0.5, rinv[:, 0:n], MUL, MUL)
                else:
                    S.tt("dve", postb[:, 1, 0:n], postb[:, 1, 0:n], rinv[:, 0:n], MUL)

        def norm_T(xb, rows, ssb, xnb, dstT, col0):
            r = slice(0, rows)
            S.act(junk[r, :], xb[r, :], AF.Square, accum=ssb[r, 0:1])
            S.act(ssb[r, 1:2], ssb[r, 0:1], AF.Sqrt, bias=EPS, scale=1.0 / D)
            S.recip(ssb[r, 1:2], ssb[r, 1:2])
            S.ts("dve", xnb[r, :], xb[r, :], ssb[r, 1:2], None, MUL)
            for k in range(8):
                S.tr(pT[:, k * rows:(k + 1) * rows], xnb[r, k * 128:(k + 1) * 128], identb[r, r])
            S.cp("act", dstT[:, :, col0:col0 + rows], pT[:, 0:8 * rows].re("p (k t) -> p k t", k=8))

        xp = io["xp"]
        n_chunk = 0
        for blk in range(nblk):
            xnTb = xnT[blk % 2]
            postb = post[blk % 2]
            zabb = zab[blk % 2]
            for t in range(4):
                tile = blk * 4 + t
                S.dma("sp", xt[tile % 2], xp[tile * 128:(tile + 1) * 128, :])
                norm_T(xt[tile % 2], 128, ss[tile % 2], xn[tile % 2], xnTb, t * 128)
            for s in range(3):
                for k in range(8):
                    S.mm(pq, W[:, k, s * 128:(s + 1) * 128], xnTb[:, k, :], start=(k == 0), stop=(k == 7))
                S.cp("act", pre[:, s, 3:515], pq)
            for t in range(4):
                for k in range(8):
                    S.mm(pz[:, 0:130], xnTb[:, k, t * 128:(t + 1) * 128], W[:, k, 384:514], start=(k == 0), stop=(k == 7))
                S.cp("dve", zabb[t], pz[:, 0:130])
            if blk == nblk - 1:
                for k in range(8):
                    S.mm(pz[0:3, 0:384], xnTb[:, k, 509:512], W[:, k, 0:384], start=(k == 0), stop=(k == 7))
                S.cp("dve", tok3[0:3, :], pz[0:3, 0:384])
                S.dma("sp", io["sq_p"], tok3[0:3, :])
            conv_norm(pre, acc, postb, 512)
            S.cp("pool", pre[:, :, 0:3], pre[:, :, 512:515])
            for ci in range(4):
                cs = slice(ci * 128, (ci + 1) * 128)
                tok0 = blk * 512 + ci * 128
                chunk(postb[:, 0, cs], postb[:, 1, cs], postb[:, 2, cs], zabb[ci], 128,
                      Sst[n_chunk % 2], Sst[(n_chunk + 1) % 2], bo[blk][ci * 128:(ci + 1) * 128, :], 6)
                n_chunk += 1
            if use_coll:
                if pend[0] is not None:
                    S.coll_relay(pend[0])
                pend[0] = S.coll_issue(bo[blk], gath[blk])
        S.dma("sp", io["sd_p"], Sst[n_chunk % 2])

        if do_samples:
            xs_t = xt[0]
            S.dma("sp", xs_t[0:32, :], io["xs"])
            xnTs = xnT[0]
            norm_T(xs_t, 32, ss[0], xn[0], xnTs, 0)
            sqin = xt[1]
            S.dma("sp", sqin[0:96, 0:384], io["sq_in"])
            pre_s = pre
            pv3 = pre_s[:, :, 3:131].re("p s (b j) -> p s b j", j=4)
            for s in range(3):
                for k in range(8):
                    S.mm(pq[:, 0:32], W[:, k, s * 128:(s + 1) * 128], xnTs[:, k, 0:32], start=(k == 0), stop=(k == 7))
                S.cp("act", pv3[:, s, :, 3], pq[:, 0:32])
                S.tr(pn[:, 0:96], sqin[0:96, s * 128:(s + 1) * 128], ident[0:96, 0:96])
                S.cp("dve", pv3[:, s, :, 0:3], pn[:, 0:96].re("p (b j) -> p b j", j=3))
            for k in range(8):
                S.mm(pz[0:32, 0:384], xnTs[:, k, 0:32], W[:, k, 0:384], start=(k == 0), stop=(k == 7))
            S.cp("dve", tok3[0:32, :], pz[0:32, 0:384])
            S.dma("sp", io["sq_s"][:, 2, :], tok3[0:32, :])
            S.dma("sp", io["sq_s"][:, 0:2, :], io["sq_in"].re("(b j) f -> b j f", j=3)[:, 1:3, :])
            post_s = post[0]
            conv_norm(pre_s, acc, post_s, 128)
            sts = [C.sb([128, 128], F32, "sts") for _ in range(4)]
            zs = [C.sb([1, 130], F32, "zs") for _ in range(2)]
            for bi in range(NS):
                col = 4 * bi + 3
                s_in = sts[(2 * bi) % 4]
                s_out = sts[(2 * bi + 1) % 4]
                S.dma("sp", s_in, io["sd_in"][bi])
                for k in range(8):
                    S.mm(pz[0:1, 0:130], xnTs[:, k, bi:bi + 1], W[:, k, 384:514], start=(k == 0), stop=(k == 7))
                zb = zs[bi % 2]
                S.cp("dve", zb, pz[0:1, 0:130])
                chunk(post_s[:, 0, col:col + 1], post_s[:, 1, col:col + 1], post_s[:, 2, col:col + 1], zb, 1,
                      s_in, s_out, bo[NBLK_A][bi:bi + 1, :], 0)
                S.dma("sp", io["sd_s"][bi], s_out)
        if pend[0] is not None:
            S.coll_relay(pend[0])
        if do_samples and use_coll:
            S.coll(bo[NBLK_A], gath[NBLK_A])
        S.flush()


def consts():
    i = np.arange(128)
    ident = np.eye(128, dtype=np.float32)
    U = (i[:, None] <= i[None, :]).astype(np.float32)
    Ms = (i[:, None] > i[None, :]).astype(np.float32)
    return {"ident": ident, "U": U, "Ms": Ms}


A_IN = [("xp", [T_P, D]), ("xs", [NS, D]), ("wqkv", [D, 384]), ("wzab", [D, 130]), ("gmix0", [128, 8]),
        ("wconv", [128, 3, 4]), ("gout", [128, 128]), ("alog", [128, 1]), ("dtb", [128, 1]),
        ("sd_in", [NS, 128, 128]), ("sq_in", [96, 384]), ("ident", [128, 128]), ("U", [128, 128]), ("Ms", [128, 128])]
A_OUT = [("sd_p", [128, 128]), ("sq_p", [3, 384]), ("sd_s", [NS, 128, 128]), ("sq_s", [NS, 3, 384])]


def prep_a(inp, h):
    f = np.ascontiguousarray
    w_in = inp["w_a_in"][0]
    hs = slice(h * 128, (h + 1) * 128)
    cols = np.concatenate([np.arange(s * 1024 + h * 128, s * 1024 + (h + 1) * 128) for s in range(3)])
    m = {}
    m["xp"] = f(inp["x_prompt"][0])
    m["xs"] = f(inp["x_sample"][:, 0])
    m["wqkv"] = f(w_in[:, cols])
    zc = np.concatenate([np.arange(3088 + h * 128, 3088 + (h + 1) * 128), [3072 + h], [3080 + h]])
    m["wzab"] = f(w_in[:, zc])
    m["gmix0"] = f(inp["g_mix_norm"][0].reshape(8, 128).T)
    m["wconv"] = f(inp["w_a_conv"][0][:, cols].reshape(4, 3, 128).transpose(2, 1, 0))
    m["gout"] = f(np.broadcast_to(inp["g_a_out_norm"][0][None, :], (128, 128)))
    m["alog"] = f(np.broadcast_to(inp["a_log"][0, h].reshape(1, 1), (128, 1)))
    m["dtb"] = f(np.broadcast_to(inp["a_dt_bias"][0, h].reshape(1, 1), (128, 1)))
    m["sd_in"] = f(inp["state_delta"][0, :, h])
    m["sq_in"] = f(inp["state_qkv_conv"][0][:, :, cols].reshape(96, 384))
    m.update(consts())
    return m


DFF = 2816
NPAIR = 22
GRP = 4


class Banks:
    def __init__(self, C, n):
        self.b = [C.ps([128, 512], F32, "bank") for _ in range(n)]
        self.i = 0

    def get(self):
        t = self.b[self.i % len(self.b)]
        self.i += 1
        return t


def phase_bc(nc, S, io, NT, og_tile_src, kv_exchange, u_exchange, prepass=True):
    NOWN = NT * 128
    NCOL = NOWN + 16
    CH, CS = NOWN, NOWN + 2
    BLK = [(c, min(c + 512, NOWN)) for c in range(0, NOWN, 512)] + [(NOWN, NCOL)]
    OWNB = BLK[:-1]
    LASTB = BLK[-1]
    MUL, ADD, SUB = ALU.mult, ALU.add, ALU.subtract
    es = ExitStack()
    with es:
        C = Ctx(nc, es)
        PS = Banks(C, 7)
        pbf = C.ps([128, 1024], BF16, "pbf")
        ident = C.sb([128, 128], F32, "ident")
        identb = C.sb([128, 128], BF16, "identb")
        onesb = C.sb([128, 128], BF16, "onesb")
        S.dma("sp", ident, io["ident"])
        S.cp("dve", identb, ident)
        S.memset("dve", onesb, 1.0)
        flag = C.sb([128, 1], F32, "flag")
        S.dma("sp", flag, io["flag"])
        io["flag_sb"] = flag
        hT = C.sb([128, 8, NCOL], F32, "hT")
        stage = [C.sb([128, 2816], F32, "stage") for _ in range(2)]
        wbuf = [C.sb([128, 2048], BF16, "wbuf") for _ in range(2)]
        gains = C.sb([128, 6, 8], F32, "gains")
        S.dma("sp", gains, io["gains"])
        wcnt = [0]

        def load_w(src_ap_T, kc, nf, gain):
            i = wcnt[0] % 2
            wcnt[0] += 1
            st = stage[i][:, 0:kc * nf].re("p (k f) -> p k f", k=kc)
            wb = wbuf[i][:, 0:kc * nf].re("p (k f) -> p k f", k=kc)
            S.dma("sp" if i == 0 else "act", st, src_ap_T)
            if gain is None:
                S.cp("pool", wb, st)
            else:
                for k in range(kc):
                    S.ts("pool" if k % 2 else "dve", wb[:, k, :], st[:, k, :], gain[:, k:k + 1], None, MUL)
            return wb

        def fm_linear(Wd, KC, F, gain, xT, consume, blocks, slab=256):
            Wv = Wd.re("(k p) f -> p k f", p=128)
            for f0 in range(0, F, slab):
                nf = min(slab, F - f0)
                wb = load_w(Wv[:, :, f0:f0 + nf], KC, nf, gain)
                for fc in range(nf // 128):
                    for bi, (c0, c1) in blocks:
                        ps = PS.get()
                        for k in range(KC):
                            S.mm(ps[:, 0:c1 - c0], wb[:, k, fc * 128:(fc + 1) * 128], xT[:, k, c0:c1],
                                 start=(k == 0), stop=(k == KC - 1))
                        consume(f0 // 128 + fc, bi, c0, c1, ps)

        def rmsnorm_fm():
            for (c0, c1) in BLK:
                n = c1 - c0
                S.act(X["sqb"][:, :, 0:n], hT[:, :, c0:c1], AF.Square)
                ps = PS.get()
                for k in range(8):
                    S.mm(ps[:, 0:n], onesb, X["sqb"][:, k, 0:n], start=(k == 0), stop=(k == 7))
                S.act(X["rs"][:, 0:n], ps[:, 0:n], AF.Sqrt, bias=EPS, scale=1.0 / D)
                S.recip(X["rs"][:, 0:n], X["rs"][:, 0:n])
                for k in range(8):
                    S.tt("dve" if k % 2 else "pool", X["hnT"][:, k, c0:c1], hT[:, k, c0:c1], X["rs"][:, 0:n], MUL)

        EB = list(enumerate(BLK))

        X = {}

        def stage_b1():
            xrow = [stage[i][:, 0:1024] for i in range(2)]
            ogrow = [wbuf[i][:, 0:1024].re("p (r d) -> p r d", r=8) for i in range(2)]
            for t in range(NT + 1):
                rows = 128 if t < NT else 16
                c0 = t * 128
                xr = xrow[t % 2]
                S.dma("sp", xr[0:rows, :], io["xh"][c0:c0 + rows, :])
                for half in range(2):
                    ps = PS.get()
                    for k in range(4):
                        kk = half * 4 + k
                        S.tr(ps[:, k * rows:(k + 1) * rows], xr[0:rows, kk * 128:(kk + 1) * 128], ident[0:rows, 0:rows])
                    S.cp("act" if half else "dve", hT[:, half * 4:half * 4 + 4, c0:c0 + rows],
                         ps[:, 0:4 * rows].re("p (k t) -> p k t", k=4))
                og = ogrow[t % 2]
                og_tile_src(S, t, og)
                pbb = pbf
                for r in range(8):
                    S.tr(pbb[:, r * rows:(r + 1) * rows], og[0:rows, r, :], identb[0:rows, 0:rows])
                S.cp("act", X["hnT"][:, :, c0:c0 + rows], pbb[:, 0:8 * rows].re("p (k t) -> p k t", k=8))

            def add_h(fc, bi, c0, c1, ps):
                S.tt("dve", hT[:, fc, c0:c1], ps[:, 0:c1 - c0], hT[:, fc, c0:c1], ADD)

            fm_linear(io["w_a_out"], 8, 1024, None, X["hnT"], add_h, EB)


        def ffn(layer, gidx):
            strow = [stage[g][0:8, 0:DFF] for g in range(2)]
            for g in range(2):
                S.dma("sp", strow[g], io["sf_in"][layer][:, g * DFF:(g + 1) * DFF])
            S.dma("sp", X["wcf"], io["wcf"][layer])
            for g in range(2):
                for f in range(NPAIR):
                    if f % 4 == 0:
                        ps = PS.get()
                    S.tr(ps[:, (f % 4) * 8:(f % 4) * 8 + 8], strow[g][:, f * 128:(f + 1) * 128], ident[0:8, 0:8])
                    if f % 4 == 3 or f == NPAIR - 1:
                        nn = f % 4 + 1
                        S.cp("dve", X["stT"][:, g, f - nn + 1:f + 1, :], ps[:, 0:nn * 8].re("p (f b) -> p f b", b=8))
            rmsnorm_fm()
            Wup = io["w_up"][layer]
            gain = gains[:, gidx, :]
            if layer == 1:
                def save_last(fcg, bi, c0, c1, ps):
                    S.cp("dve", X["usave"][:, fcg % 2, fcg // 2, 0:2], ps[:, 0:2])
                if prepass:
                    fm_linear(Wup, 8, 2 * DFF, gain, X["hnT"], save_last, [(0, (NOWN - 2, NOWN))])
                u_exchange(S, X["usave"], X["uprev"])
                for g in range(2):
                    S.ts("dve", X["uprev"][:, g], X["uprev"][:, g], flag[:, 0:1], None, MUL)
            blocks = [(len(BLK) - 1, LASTB)] + list(enumerate(OWNB))
            for f0 in range(0, NPAIR, GRP):
                npair = min(GRP, NPAIR - f0)
                for fi in range(npair):
                    f = f0 + fi
                    prev_ub = [None, None]

                    def consume(fcg, bi, c0, c1, ps, f=f, fi=fi, prev_ub=prev_ub):
                        g = fcg % 2
                        n = c1 - c0
                        w = X["wcf"][:, f, g, :]
                        if bi == len(BLK) - 1:
                            S.cp("dve", X["ulast"][:, g, :], ps[:, 0:16])
                            st = X["stT"][:, g, f, :].re("p (b j) -> p b j", j=2)
                            S.ts("dve", X["cvs"][:, g, :], st[:, :, 0], w[:, 0:1], None, MUL)
                            S.stt("dve", X["cvs"][:, g, :], st[:, :, 1], w[:, 1:2], X["cvs"][:, g, :], MUL, ADD)
                            S.stt("dve", X["cvs"][:, g, :], X["ulast"][:, g, 2:6], w[:, 2:3], X["cvs"][:, g, :], MUL, ADD)
                            S.cp("pool", X["usave"][:, g, f, 2:6], X["ulast"][:, g, 2:6])
                            if g == 0:
                                S.act(X["sgate"][:, CS:CS + 4], X["cvs"][:, 0, :], AF.Silu)
                            else:
                                S.memset("pool", X["actb"][:, fi, c0:c1], 0.0)
                                S.tt("dve", X["actb"][:, fi, CS:CS + 4], X["sgate"][:, CS:CS + 4], X["cvs"][:, 1, :], MUL)
                            return
                        u = X["ub"][g][bi % 2]
                        S.cp("act", u[:, 2:2 + n], ps[:, 0:n])
                        if bi == 0:
                            if layer == 0:
                                S.cp("pool", u[:, 0:2], X["ulast"][:, g, 0:2])
                            else:
                                S.cp("pool", u[:, 0:2], X["uprev"][:, g, f, :])
                        else:
                            S.cp("pool", u[:, 0:2], prev_ub[g][:, 512:514])
                        prev_ub[g] = u
                        if bi == len(OWNB) - 1:
                            S.cp("pool", X["usave"][:, g, f, 0:2], u[:, n:n + 2])
                        cv = X["cvu"][bi % 2]
                        S.ts("dve", cv[:, 0:n], u[:, 0:n], w[:, 0:1], None, MUL)
                        S.stt("dve", cv[:, 0:n], u[:, 1:n + 1], w[:, 1:2], cv[:, 0:n], MUL, ADD)
                        S.stt("dve", cv[:, 0:n], u[:, 2:n + 2], w[:, 2:3], cv[:, 0:n], MUL, ADD)
                        if g == 0:
                            S.act(X["sgate"][:, c0:c1], cv[:, 0:n], AF.Silu)
                        else:
                            S.tt("pool", X["actb"][:, fi, c0:c1], X["sgate"][:, c0:c1], cv[:, 0:n], MUL)

                    Wv = Wup.re("(k p) f -> p k f", p=128)
                    wb = load_w(Wv[:, :, f * 256:(f + 1) * 256], 8, 256, gain)
                    for fc in range(2):
                        for bi, (c0, c1) in blocks:
                            ps = PS.get()
                            for k in range(8):
                                S.mm(ps[:, 0:c1 - c0], wb[:, k, fc * 128:(fc + 1) * 128], X["hnT"][:, k, c0:c1],
                                     start=(k == 0), stop=(k == 7))
                            consume(fc, bi, c0, c1, ps)
                Wd = io["w_down"][layer]
                for half in range(2):
                    wb = load_w(Wd[f0 * 128:(f0 + npair) * 128, half * 512:(half + 1) * 512].re("(k p) f -> p k f", p=128),
                                npair, 512, None)
                    for fo4 in range(4):
                        fo = half * 4 + fo4
                        for bi, (c0, c1) in EB:
                            ps = PS.get()
                            for k in range(npair):
                                S.mm(ps[:, 0:c1 - c0], wb[:, k, fo4 * 128:(fo4 + 1) * 128], X["actb"][:, k, c0:c1],
                                     start=(k == 0), stop=(k == npair - 1))
                            S.tt("dve", hT[:, fo, c0:c1], ps[:, 0:c1 - c0], hT[:, fo, c0:c1], ADD)
            for g in range(2):
                sfrow = stage[g][0:6, 0:DFF]
                for f in range(NPAIR):
                    if f % 4 == 0:
                        ps = PS.get()
                    S.tr(ps[0:6, (f % 4) * 128:(f % 4 + 1) * 128], X["usave"][:, g, f, :], ident)
                    if f % 4 == 3 or f == NPAIR - 1:
                        nn = f % 4 + 1
                        S.cp("dve", sfrow[:, (f - nn + 1) * 128:(f + 1) * 128], ps[0:6, 0:nn * 128])
                S.dma("sp", io["sf_out"][layer][:, g * DFF:(g + 1) * DFF], sfrow)
            S.dma("sp", io["sf_old"][layer], io["sf_in"][layer].re("(b j) f -> b j f", j=2)[:, 1, :])
            if layer == 1 and io.get("post_ffn1") is not None:
                io["post_ffn1"](S, X["usave"])

        prow = [stage[i][:, 0:256] for i in range(2)]

        def ple(layer, gidx):
            for t in range(NT + 1):
                rows = 128 if t < NT else 16
                c0 = t * 128
                pr = prow[t % 2]
                S.dma("sp", pr[0:rows, :], io["pp"][layer][c0:c0 + rows, :])
                ps = PS.get()
                for k in range(2):
                    S.tr(ps[:, k * rows:(k + 1) * rows], pr[0:rows, k * 128:(k + 1) * 128], ident[0:rows, 0:rows])
                S.cp("dve", X["ppT"][:, :, c0:c0 + rows], ps[:, 0:2 * rows].re("p (k t) -> p k t", k=2))
            rmsnorm_fm()
            Wg = io["w_pg"][layer].re("(k p) f -> p k f", p=128)
            Wp = io["w_pp"][layer].re("(k p) f -> p k f", p=128)
            n = 0
            for f0 in range(0, 1024, 256):
                wg = load_w(Wg[:, :, f0:f0 + 256], 8, 256, gains[:, gidx, :])
                wp = load_w(Wp[:, :, f0:f0 + 256], 2, 256, None)
                for fc in range(2):
                    fo = f0 // 128 + fc
                    for bi, (c0, c1) in EB:
                        m = c1 - c0
                        pg = PS.get()
                        for k in range(8):
                            S.mm(pg[:, 0:m], wg[:, k, fc * 128:(fc + 1) * 128], X["hnT"][:, k, c0:c1], start=(k == 0), stop=(k == 7))
                        pq_ = PS.get()
                        for k in range(2):
                            S.mm(pq_[:, 0:m], wp[:, k, fc * 128:(fc + 1) * 128], X["ppT"][:, k, c0:c1], start=(k == 0), stop=(k == 1))
                        sg = X["sgb"][n % 2]
                        n += 1
                        S.act(sg[:, 0:m], pg[:, 0:m], AF.Sigmoid)
                        S.tt("dve", sg[:, 0:m], pq_[:, 0:m], sg[:, 0:m], MUL)
                        S.tt("pool", hT[:, fo, c0:c1], hT[:, fo, c0:c1], sg[:, 0:m], ADD)


        def alloc_norm(C2):
            X["hnT"] = C2.sb([128, 8, NCOL], BF16, "hnT")
            X["sqb"] = C2.sb([128, 8, 512], BF16, "sqb")
            X["rs"] = C2.sb([128, 512], F32, "rs")

        def alloc_ffn(C2):
            alloc_norm(C2)
            X["actb"] = C2.sb([128, GRP, NCOL], BF16, "actb")
            X["ppT"] = X["actb"][:, 0:2, :]
            X["stT"] = C2.sb([128, 2, NPAIR, 8], F32, "stT")
            X["wcf"] = C2.sb([128, NPAIR, 2, 3], F32, "wcf")
            X["usave"] = C2.sb([128, 2, NPAIR, 6], F32, "usave")
            X["uprev"] = C2.sb([128, 2, NPAIR, 2], F32, "uprev")
            X["ulast"] = C2.sb([128, 2, 16], F32, "ulast")
            X["cvs"] = C2.sb([128, 2, 4], F32, "cvs")
            X["ub"] = [[C2.sb([128, 514], F32, "ub") for _ in range(2)] for _ in range(2)]
            X["sgate"] = C2.sb([128, NCOL], F32, "sgate")
            X["cvu"] = [C2.sb([128, 512], F32, "cvu") for _ in range(2)]
            X["sgb"] = [C2.sb([128, 512], F32, "sgb") for _ in range(2)]

        def tm_project(C2, Wd, gidx, nh_norm, nh_tot, gain_rows, sink):
            F = nh_tot * 128
            Wb = C2.sb([128, 8, F], BF16, "Wb")
            Wv = Wd.re("(k p) f -> p k f", p=128)
            for f0 in range(0, F, 256):
                i = wcnt[0] % 2
                wcnt[0] += 1
                st = stage[i][:, 0:2048].re("p (k f) -> p k f", k=8)
                S.dma("sp" if i == 0 else "act", st, Wv[:, :, f0:f0 + 256])
                for k in range(8):
                    S.ts("pool" if k % 2 else "dve", Wb[:, k, f0:f0 + 256], st[:, k, :], gains[:, gidx, k:k + 1], None, MUL)
            gr = C2.sb([128, nh_norm, 128], F32, "gr")
            S.dma("sp", gr, gain_rows)
            of = [C2.sb([128, nh_tot, 128], F32, "of") for _ in range(2)]
            sqk = C2.sb([128, nh_norm, 128], F32, "sqk")
            red = C2.sb([128, 16], F32, "red")
            cs = [C2.sb([128, 2, nh_norm, 16], F32, "cs") for _ in range(2)]
            tr_ = [C2.sb([128, nh_norm, 16], F32, "tr") for _ in range(4)]
            for t in range(NT + 1):
                rows = 128 if t < NT else 16
                r = slice(0, rows)
                c0 = t * 128
                o = of[t % 2]
                for f0 in range(0, F, 512):
                    ps = PS.get()
                    for k in range(8):
                        S.mm(ps[r, :], X["hnT"][:, k, c0:c0 + rows], Wb[:, k, f0:f0 + 512], start=(k == 0), stop=(k == 7))
                    S.cp("act" if (f0 // 512) % 2 else "dve", o[r].re("p h d -> p (h d)")[:, f0:f0 + 512], ps[r, :])
                kn = o[r, 0:nh_norm, :]
                S.tt("pool", sqk[r], kn, kn, MUL)
                S.op("dve", (lambda e, o_=red[r, 0:nh_norm].ap, i_=sqk[r].ap: e.tensor_reduce(o_, i_, mybir.AxisListType.X, ALU.add)),
                     [sqk.buf], [red.buf])
                S.act(red[r, 0:nh_norm], red[r, 0:nh_norm], AF.Sqrt, bias=EPS, scale=1.0 / HD)
                S.recip(red[r, 0:nh_norm], red[r, 0:nh_norm])
                for h in range(nh_norm):
                    S.stt("dve", o[r, h, :], o[r, h, :], red[r, h:h + 1], gr[r, h, :], MUL, MUL)
                cst = cs[t % 2]
                S.dma("act", cst[r], io["rope"][c0:c0 + rows, :, 0:nh_norm, :])
                x1, x2 = o[r, 0:nh_norm, 0:16], o[r, 0:nh_norm, 16:32]
                co, si = cst[r, 0], cst[r, 1]
                S.tt("pool", tr_[0][r], x1, co, MUL)
                S.tt("pool", tr_[1][r], x2, si, MUL)
                S.tt("pool", tr_[2][r], x2, co, MUL)
                S.tt("pool", tr_[3][r], x1, si, MUL)
                S.tt("dve", x1, tr_[0][r], tr_[1][r], SUB)
                S.tt("dve", x2, tr_[2][r], tr_[3][r], ADD)
                sink(t, rows, c0, o)

        def stage_kv(C2):
            alloc_norm(C2)
            rmsnorm_fm()
            kb = [C2.sb([128, 12, 128], BF16, "kb") for _ in range(2)]
            kvt = [C2.sb([128, 3, 512], BF16, "kvt") for _ in range(2)]

            def sink(t, rows, c0, o):
                r = slice(0, rows)
                S.dma("sp", io["kv_out"][c0:c0 + rows].re("t a h d -> t (a h) d"), o[r])
                if t == NT:
                    S.dma("sp", io["kvs_scr"], o[0:16].re("p h d -> p (h d)"))
                    return
                b = kb[t % 2]
                S.cp("act", b, o)
                kt = kvt[t % 2]
                for h in range(6):
                    S.tr(pbf[:, h * 128:(h + 1) * 128], b[:, h, :], identb)
                for g in range(3):
                    S.cp("dve", kt[:, g, 0:256], pbf[:, g * 256:(g + 1) * 256])
                    S.cp("pool", kt[:, g, 256:512], b[:, 6 + 2 * g:8 + 2 * g, :].re("p h d -> p (h d)"))
                S.dma("sp", io["kvt_own"][:, t].re("g p f -> p g f"), kt)

            tm_project(C2, io["w_kv"], 2, 6, 12, io["gk_rows"], sink)

        def stage_q(C2):
            alloc_norm(C2)
            rmsnorm_fm()
            qb = [C2.sb([128, 12, 128], BF16, "qb") for _ in range(2)]

            def sink(t, rows, c0, o):
                r = slice(0, rows)
                b = qb[t % 2]
                S.cp("act", b[r], o[r])
                S.dma("sp", io["q_scr"][c0:c0 + rows], b[r].re("p h d -> p (h d)"))

            tm_project(C2, io["w_q"], 3, 12, 12, io["gq_rows"], sink)

        SPAN = (1, 4, 16)
        SCALE = HD ** -0.5

        def stage_att(C2):
            NPREV = sum(SPAN)
            kvo = C2.sb([128, 3, NT, 514], BF16, "kvo")
            kvp = C2.sb([128, NPREV, 514], BF16, "kvp")
            masks = C2.sb([128, 16, 256], BF16, "masks")
            S.dma("sp", masks, io["masks"].re("m p f -> p m f"))
            S.memset("dve", kvo, 1.0)
            S.memset("pool", kvp, 1.0)
            for g in range(3):
                src = io["kvt_own"][g].re("t p f -> p t f")
                S.dma("sp", kvo[:, g, :, 0:256], src[:, :, 0:256])
                for h in range(2):
                    S.dma("act", kvo[:, g, :, 256 + h * 129:256 + h * 129 + 128], src[:, :, 256 + h * 128:256 + (h + 1) * 128])
            kv_exchange(S, kvp)
            wo = C2.sb([128, 4, 1024], BF16, "wo")
            for half in range(2):
                st = stage[half][:, 0:2048].re("p (k f) -> p k f", k=4)
                S.dma("sp", st, io["w_o"].re("(k p) f -> p k f", p=128)[:, :, half * 512:(half + 1) * 512])
                S.cp("pool", wo[:, :, half * 512:(half + 1) * 512], st)
            onec = C2.sb([128, 1], BF16, "onec")
            S.memset("dve", onec, 1.0)
            qrow = [C2.sb([128, 1536], BF16, "qrow") for _ in range(2)]
            qT = [C2.sb([128, 12, 128], BF16, "qT") for _ in range(2)]
            pT = [C2.sb([128, 256], BF16, "pT") for _ in range(4)]
            rden = C2.sb([128, 4], F32, "rden")
            ob = C2.sb([128, 4, 128], BF16, "ob")
            oT = C2.sb([128, 4, 128], BF16, "oT")
            accb = [PS.b.pop() for _ in range(4)]
            moff = (0, 2, 5)
            npt = 0
            for q in range(NT):
                qr = qrow[q % 2]
                S.dma("act", qr, io["q_scr"][q * 128:(q + 1) * 128])
                qt = qT[q % 2]
                for hh in range(0, 12, 8):
                    nh = min(8, 12 - hh)
                    for h in range(nh):
                        S.tr(pbf[:, h * 128:(h + 1) * 128], qr[:, (hh + h) * 128:(hh + h + 1) * 128], identb)
                    S.cp("dve", qt[:, hh:hh + nh, :], pbf[:, 0:nh * 128].re("p (h t) -> p h t", h=nh))
                combos = [(g, dl) for g in range(3) for dl in range(SPAN[g], -1, -1)]
                for ci, (g, dl) in enumerate(combos):
                    ti = q - dl
                    if ti >= 0:
                        kt = kvo[:, g, ti, :]
                        pv = 0
                    else:
                        kt = kvp[:, sum(SPAN[:g]) + ti + SPAN[g], :]
                        pv = 8
                    mi = moff[g] + (0 if dl == 0 else (2 if dl == SPAN[g] else 1))
                    if g == 0 and dl == 1:
                        mi = 1
                    for G in range(2):
                        ps = PS.get()
                        S.mm(ps[:, 0:256], kt[:, G * 128:(G + 1) * 128],
                             qt[:, g * 4 + G * 2:g * 4 + G * 2 + 2, :].re("p e t -> p (e t)"))
                        p = pT[npt % 4]
                        npt += 1
                        S.act(p, ps[:, 0:256], AF.Exp, scale=SCALE)
                        S.tt("pool", p, p, masks[:, pv + mi, :], MUL)
                        for E in range(2):
                            S.mm(accb[G * 2 + E][:, 0:129], p[:, E * 128:(E + 1) * 128], kt[:, 256 + G * 129:256 + (G + 1) * 129],
                                 start=(ci == 0), stop=(ci == len(combos) - 1))
                for G in range(2):
                    for E in range(2):
                        acc = accb[G * 2 + E][:, 0:129]
                        S.recip(rden[:, G * 2 + E:G * 2 + E + 1], acc[:, 128:129])
                        S.ts("dve", ob[:, G * 2 + E, :], acc[:, 0:128], rden[:, G * 2 + E:G * 2 + E + 1], None, MUL)
                for h in range(4):
                    S.tr(pbf[:, h * 128:(h + 1) * 128], ob[:, h, :], identb)
                S.cp("act", oT, pbf[:, 0:512].re("p (h t) -> p h t", h=4))
                for fo in range(8):
                    ps = PS.get()
                    for k in range(4):
                        S.mm(ps[:, 0:128], wo[:, k, fo * 128:(fo + 1) * 128], oT[:, k, :], start=(k == 0), stop=(k == 3))
                    S.tt("dve", hT[:, fo, q * 128:(q + 1) * 128], ps[:, 0:128], hT[:, fo, q * 128:(q + 1) * 128], ADD)

            PS.b.extend(accb)


        def stage_att_s(C2):
            accb = [PS.b.pop() for _ in range(2)]
            wo = C2.sb([128, 4, 1024], BF16, "wo")
            for half in range(2):
                st = stage[half][:, 0:2048].re("p (k f) -> p k f", k=4)
                S.dma("sp", st, io["w_o"].re("(k p) f -> p k f", p=128)[:, :, half * 512:(half + 1) * 512])
                S.cp("pool", wo[:, :, half * 512:(half + 1) * 512], st)
            qs = C2.sb([16, 1536], BF16, "qs")
            S.dma("sp", qs, io["q_scr"][NOWN:NOWN + 16])
            qsT = C2.sb([128, 12, 16], F32, "qsT")
            qsf = C2.sb([16, 1536], F32, "qsf")
            S.cp("dve", qsf, qs)
            for hh in range(0, 12, 4):
                ps = PS.get()
                for h in range(4):
                    S.tr(ps[:, h * 16:(h + 1) * 16], qsf[:, (hh + h) * 128:(hh + h + 1) * 128], ident[0:16, 0:16])
                S.cp("dve", qsT[:, hh:hh + 4, :], ps[:, 0:64].re("p (h t) -> p h t", h=4))
            oneF = C2.sb([128, 1], F32, "oneF")
            S.memset("dve", oneF, 1.0)
            ct = [C2.sb([128, 512], F32, "ct") for _ in range(2)]
            ktS = [C2.sb([128, 128], F32, "ktS") for _ in range(2)]
            srow = [C2.sb([1, 1536], F32, "srow") for _ in range(2)]
            knT = C2.sb([128, 1], F32, "knT")
            pS = [C2.sb([128, 2], F32, "pS") for _ in range(2)]
            pself = C2.sb([1, 2], F32, "pself")
            osb = C2.sb([2, 130], F32, "osb")
            vS = [C2.sb([128, 129], F32, "vS") for _ in range(2)]
            vself = [C2.sb([1, 129], F32, "vself") for _ in range(2)]
            for i_ in range(2):
                S.memset("dve", vS[i_], 1.0)
                S.memset("dve", vself[i_], 1.0)
            oTs = C2.sb([128, 4, 16], BF16, "oTs")
            S.memset("dve", oTs, 0.0)
            DIL = (1, 4, 16)
            nu = 0
            for b in range(4):
                sr = srow[b % 2]
                S.dma("sp", sr, io["kvs_scr"][2 + b:3 + b, :])
                for G in range(2):
                    acc = accb[G][0:2, 0:129]
                    for g in range(3):
                        c = ct[nu % 2]
                        nu += 1
                        cache = io["cache"][g]
                        Wb_ = cache.ap.shape[1]
                        src = cache[b].re("(m d) a h c -> m d (a h c)", d=DIL[g])[:, 0, :]
                        S.dma("sp" if nu % 2 else "act", c, src)
                        ps = PS.get()
                        S.tr(ps[:, 0:128], c[:, G * 128:(G + 1) * 128], ident)
                        kts = ktS[nu % 2]
                        S.cp("dve", kts, ps[:, 0:128])
                        qcols = qsT[:, g * 4 + G * 2:g * 4 + G * 2 + 2, 2 + b]
                        ps2 = PS.get()
                        S.mm(ps2[:, 0:2], kts, qcols)
                        p = pS[nu % 2]
                        S.act(p, ps2[:, 0:2], AF.Exp, scale=SCALE)
                        vs_ = vS[nu % 2]
                        S.cp("pool", vs_[:, 0:128], c[:, 256 + G * 128:256 + (G + 1) * 128])
                        S.mm(acc, p, vs_, start=(g == 0), stop=False)
                        hk = 2 * g + G
                        ps3 = PS.get()
                        S.mm(ps3[:, 0:1], sr[0:1, hk * 128:(hk + 1) * 128], oneF[0:1, 0:1])
                        S.cp("dve", knT, ps3[:, 0:1])
                        ps4 = PS.get()
                        S.mm(ps4[0:1, 0:2], knT, qcols)
                        S.act(pself, ps4[0:1, 0:2], AF.Exp, scale=SCALE)
                        vf_ = vself[nu % 2]
                        S.cp("pool", vf_[:, 0:128], sr[0:1, (6 + hk) * 128:(7 + hk) * 128])
                        S.mm(acc, pself, vf_, start=False, stop=(g == 2))
                    S.cp("dve", osb[:, 0:129], acc)
                    S.recip(osb[:, 129:130], osb[:, 128:129])
                    S.ts("dve", osb[:, 0:128], osb[:, 0:128], osb[:, 129:130], None, MUL)
                    ps5 = PS.get()
                    S.tr(ps5[:, 0:2], osb[:, 0:128], ident[0:2, 0:2])
                    S.cp("dve", oTs[:, G * 2:G * 2 + 2, 2 + b], ps5[:, 0:2])
            for fo in range(8):
                ps = PS.get()
                for k in range(4):
                    S.mm(ps[:, 0:16], wo[:, k, fo * 128:(fo + 1) * 128], oTs[:, k, :], start=(k == 0), stop=(k == 3))
                S.tt("dve", hT[:, fo, NOWN:NCOL], ps[:, 0:16], hT[:, fo, NOWN:NCOL], ADD)
            PS.b.extend(accb)

        def stage_out():
            yrow = [stage[i][:, 0:1024] for i in range(2)]
            for t in range(NT + 1):
                rows = 128 if t < NT else 16
                c0 = t * 128
                yr = yrow[t % 2]
                for half in range(2):
                    ps = PS.get()
                    for k in range(4):
                        S.tr(ps[0:rows, k * 128:(k + 1) * 128], hT[:, half * 4 + k, c0:c0 + rows], ident)
                    S.cp("act" if half else "dve", yr[0:rows, half * 512:(half + 1) * 512], ps[0:rows, :])
                S.dma("sp", io["y_out"][c0:c0 + rows, :], yr[0:rows, :])

        upto = io.get("upto", 99)
        with ExitStack() as es2:
            alloc_ffn(Ctx(nc, es2))
            stage_b1()
            ffn(0, 0)
            ple(0, 1)
            if io.get("dbg_h") is not None and upto == 1:
                S.dma("sp", io["dbg_h"], hT)
            S.flush()
        if upto >= 2:
            with ExitStack() as es2:
                stage_kv(Ctx(nc, es2))
                S.flush()
            with ExitStack() as es2:
                stage_q(Ctx(nc, es2))
                S.flush()
        if upto >= 3:
            with ExitStack() as es2:
                stage_att(Ctx(nc, es2))
                S.flush()
            with ExitStack() as es2:
                stage_att_s(Ctx(nc, es2))
                if io.get("dbg_h") is not None and upto == 3:
                    S.dma("sp", io["dbg_h"], hT)
                S.flush()
        if upto >= 4:
            with ExitStack() as es2:
                alloc_ffn(Ctx(nc, es2))
                ffn(1, 4)
                ple(1, 5)
                stage_out()
                S.flush()


def ffn_perm():
    idx = []
    for f in range(NPAIR):
        idx += list(range(f * 128, (f + 1) * 128)) + list(range(DFF + f * 128, DFF + (f + 1) * 128))
    return np.array(idx)


def prep_b(inp, c, NT, tok0=None):
    f = np.ascontiguousarray
    NOWN = NT * 128
    NCOL = NOWN + 16
    s = c * NOWN if tok0 is None else tok0
    x = inp["x_prompt"][0]
    m = {}
    xh = np.zeros((NCOL, D), np.float32)
    xh[0:NOWN] = x[s:s + NOWN]
    if s >= 2:
        xh[NOWN:NOWN + 2] = x[s - 2:s]
    xh[NOWN + 2:NOWN + 6] = inp["x_sample"][4 * c:4 * c + 4, 0]
    m["xh"] = xh
    pp = np.zeros((2, NCOL, 256), np.float32)
    pp[:, 0:NOWN] = inp["p_prompt"][:, 0, s:s + NOWN]
    pp[:, NOWN + 2:NOWN + 6] = inp["p_sample"][:, 4 * c:4 * c + 4, 0]
    m["pp"] = pp
    g = [inp["g_ffn_norm"][0], inp["g_ple_norm"][0], inp["g_kv_norm"], inp["g_mix_norm"][1], inp["g_ffn_norm"][1], inp["g_ple_norm"][1]]
    m["gains"] = f(np.stack([a.reshape(8, 128).T for a in g], 1))
    m["w_a_out"] = f(inp["w_a_out"][0])
    perm = ffn_perm()
    m["w_up"] = f(inp["w_ffn_up"][:, :, perm])
    m["w_down"] = f(inp["w_ffn_down"])
    m["w_pg"] = f(inp["w_ple_gate"])
    m["w_pp"] = f(inp["w_ple_proj"])
    wc = inp["w_ffn_conv"]
    m["wcf"] = f(wc.reshape(2, 3, 2, NPAIR, 128).transpose(0, 4, 3, 2, 1))
    m["sf_in"] = f(inp["state_ffn_conv"][:, 4 * c:4 * c + 4].reshape(2, 8, 2 * DFF))
    m["flag"] = np.full((128, 1), 0.0 if s == 0 else 1.0, np.float32)
    m["ident"] = np.eye(128, dtype=np.float32)
    return m


B_IN = lambda NT: [("xh", [NT * 128 + 16, D]), ("pp", [2, NT * 128 + 16, 256]), ("gains", [128, 6, 8]), ("w_a_out", [D, D]),
                   ("w_up", [2, D, 2 * DFF]), ("w_down", [2, DFF, D]), ("w_pg", [2, D, D]), ("w_pp", [2, 256, D]),
                   ("wcf", [2, 128, NPAIR, 2, 3]), ("sf_in", [2, 8, 2 * DFF]), ("flag", [128, 1]), ("ident", [128, 128])]
B_OUT = lambda NT: [("sf_out", [2, 6, 2 * DFF]), ("sf_old", [2, 4, 2 * DFF])]


GROUPS = ((128, 1), (512, 4), (2048, 16))


def att_masks(flag):
    j = np.arange(128)[:, None]
    i = np.arange(128)[None, :]
    out = []
    for (W, d) in GROUPS:
        span = W // 128
        for dl in ([0, 1] if span == 1 else [0, 1, span]):
            dist = 128 * dl + i - j
            m = ((dist >= 0) & (dist <= W) & (dist % d == 0)).astype(np.float32)
            out.append(np.concatenate([m, m], 1))
    own = np.stack(out, 0)
    return np.concatenate([own, own * flag], 0).astype(ml_dtypes.bfloat16)


def rope_tab(pos):
    half = 16
    inv = (np.float32(500000.0) ** (-np.arange(half, dtype=np.float32) * np.float32(2.0) / np.float32(32))).astype(np.float32)
    ang = pos.astype(np.float32)[:, None] * inv[None, :]
    t = np.stack([np.cos(ang), np.sin(ang)], 1).astype(np.float32)
    return np.ascontiguousarray(np.broadcast_to(t[:, :, None, :], (len(pos), 2, 12, 16)))


def prep_c(inp, c, NT, tok0=None):
    f = np.ascontiguousarray
    NOWN = NT * 128
    NCOL = NOWN + 16
    s = c * NOWN if tok0 is None else tok0
    m = {}
    pos = np.zeros(NCOL, np.int64)
    pos[0:NOWN] = s + np.arange(NOWN)
    pos[NOWN:NOWN + 2] = [max(s - 2, 0), max(s - 1, 0)]
    pos[NOWN + 2:NOWN + 6] = 16384
    m["rope"] = rope_tab(pos)
    m["gk_rows"] = f(np.broadcast_to(inp["g_k_norm"][None, None, :], (128, 6, 128)))
    m["gq_rows"] = f(np.broadcast_to(inp["g_q_norm"][0][None, None, :], (128, 12, 128)))
    m["w_kv"] = f(inp["w_kv"])
    m["w_q"] = f(inp["w_q"][0])
    m["w_o"] = f(inp["w_o"][0])
    m["masks"] = att_masks(0.0 if s == 0 else 1.0)
    cs = (c % 8)
    for g, nm in enumerate(("cache_kv_w128", "cache_kv_w512", "cache_kv_w2048")):
        m["cache%d" % g] = f(inp[nm][4 * cs:4 * cs + 4])
    return m


C_IN = lambda NT: [("rope", [NT * 128 + 16, 2, 12, 16], F32), ("gk_rows", [128, 6, 128], F32), ("gq_rows", [128, 12, 128], F32),
                   ("w_kv", [D, 1536], F32), ("w_q", [D, 1536], F32), ("w_o", [512, D], F32), ("masks", [16, 128, 256], BF16),
                   ("cache0", [4, 128, 2, 2, 128], F32), ("cache1", [4, 512, 2, 2, 128], F32), ("cache2", [4, 2048, 2, 2, 128], F32)]
C_OUT = lambda NT: [("kv_out", [NT * 128 + 16, 2, 6, 128]), ("y_out", [NT * 128 + 16, D])]
C_SCR = lambda NT: [("kvs_scr", [16, 1536], F32), ("kvt_own", [3, NT, 128, 512], BF16), ("q_scr", [NT * 128 + 16, 1536], BF16)]


NT_FULL = 16
SPAN_ = (1, 4, 16)
NPIECE_KV = sum(SPAN_)

A_STACK = ("wqkv", "wzab", "wconv", "alog", "dtb", "sd_in", "sq_in")
BC_STACK = ("xh", "pp", "sf_in", "flag", "rope", "masks", "cache0", "cache1", "cache2")


def build_program():
    NT = NT_FULL
    NOWN = NT * 128
    NCOL = NOWN + 16
    nc = bass.Bass("TRN2", target_bir_lowering=False)
    es = ExitStack()
    with es:
        S = Sched(nc, es)
        io = {}

        def mk(n, shp, kind, dt=F32):
            return T(nc.dram_tensor(n, list(shp), dt, kind=kind).ap())

        for n, shp in A_IN:
            io[n] = mk(n, ([NCORES] + shp) if n in A_STACK else shp, "ExternalInput")
        for n, shp in A_OUT:
            io[n] = mk(n, [NCORES] + shp, "ExternalOutput")
        for n, shp in B_IN(NT):
            if n not in io:
                io[n] = mk(n, ([NCORES] + shp) if n in BC_STACK else shp, "ExternalInput")
        for n, shp, dt in C_IN(NT):
            io[n] = mk(n, ([NCORES] + shp) if n in BC_STACK else shp, "ExternalInput", dt)
        for n, shp in B_OUT(NT) + C_OUT(NT):
            io[n] = mk(n, [NCORES] + shp, "ExternalOutput")
        kvs_scr = mk("kvs_scr", [16, 1536], "Internal")
        q_scr = mk("q_scr", [NCOL, 1536], "Internal", BF16)
        kvt_all = [mk("kvt_own%d" % c, [3, NT, 128, 512], "Internal", BF16) for c in range(NCORES)]
        uscr = [mk("uscr%d" % c, [128, 128], "Internal") for c in range(NCORES)]
        go_t = nc.dram_tensor("go", [NBLK_A + 1, NCORES * 512, 128], BF16, kind="Internal").ap()

        for h in range(NCORES):
            io_h = dict(io)
            for n in A_STACK:
                io_h[n] = io[n][h]
            for n, _ in A_OUT:
                io_h[n] = io[n][h]
            bo = [T(go_t[p][h * 512:(h + 1) * 512]) for p in range(NBLK_A + 1)]
            phase_a(nc, S, io_h, bo, None, use_coll=False)

        ucont = T(es.enter_context(nc.sbuf_tensor("ucont", [128, 128], F32))[:])
        for c in range(NCORES):
            io_c = dict(io)
            for n in BC_STACK:
                io_c[n] = io[n][c]
            for n, _ in B_OUT(NT) + C_OUT(NT):
                io_c[n] = io[n][c]
            for k in ("w_up", "w_down", "w_pg", "w_pp", "wcf"):
                io_c[k] = [io[k][0], io[k][1]]
            for k in ("pp", "sf_in", "sf_out", "sf_old"):
                io_c[k] = [io_c[k][0], io_c[k][1]]
            io_c["cache"] = [io_c["cache0"], io_c["cache1"], io_c["cache2"]]
            io_c["kvs_scr"], io_c["q_scr"], io_c["kvt_own"] = kvs_scr, q_scr, kvt_all[c]

            def og_src(S, t, og, c=c):
                if t < NT:
                    row0 = (t % 4) * 128
                    S.dma("sp", og, T(go_t[4 * c + t // 4].rearrange("(r n) d -> n r d", r=NCORES)[row0:row0 + 128]))
                    return
                S.memset("dve", og[0:16], 0.0)
                if c > 0:
                    S.dma("sp", og[0:2], T(go_t[4 * c - 1].rearrange("(r n) d -> n r d", r=NCORES)[510:512]))
                S.dma("sp", og[2:6], T(go_t[NBLK_A].rearrange("(r n) d -> n r d", r=NCORES)[4 * c:4 * c + 4]))

            def kv_ex(S, kvp, c=c):
                if c == 0:
                    return
                base = 0
                for g in range(3):
                    src = kvt_all[c - 1][g][NT - SPAN_[g]:NT].re("t p f -> p t f")
                    dst = kvp[:, base:base + SPAN_[g], :]
                    S.dma("sp", dst[:, :, 0:256], src[:, :, 0:256])
                    for h in range(2):
                        S.dma("act", dst[:, :, 256 + h * 129:256 + h * 129 + 128], src[:, :, 256 + h * 128:256 + (h + 1) * 128])
                    base += SPAN_[g]

            def u_ex(S, usave, uprev, c=c):
                if c == 0:
                    S.memset("dve", uprev, 0.0)
                    return
                S.dma("sp", ucont, uscr[c - 1])
                for g in range(2):
                    S.cp("dve", uprev[:, g], ucont[:, g * 2 * NPAIR:(g + 1) * 2 * NPAIR].re("p (f j) -> p f j", j=2))

            def post_ffn1(S, usave, c=c):
                S.memset("dve", ucont, 0.0)
                for g in range(2):
                    S.cp("dve", ucont[:, g * 2 * NPAIR:(g + 1) * 2 * NPAIR].re("p (f j) -> p f j", j=2), usave[:, g, :, 0:2])
                S.dma("sp", uscr[c], ucont)

            io_c["post_ffn1"] = post_ffn1
            io_c["upto"] = 99
            phase_bc(nc, S, io_c, NT, og_src, kv_ex, u_ex, prepass=False)
        S.flush(final=True)
    return nc


def kernel(**inputs):
    inp = {k: np.asarray(v) for k, v in inputs.items()}
    NT = NT_FULL
    NOWN = NT * 128
    nc = build_program()
    pa = [prep_a(inp, h) for h in range(NCORES)]
    m = dict(pa[0])
    for n in A_STACK:
        m[n] = np.ascontiguousarray(np.stack([p[n] for p in pa], 0))
    pb = [dict(prep_b(inp, c, NT), **prep_c(inp, c, NT)) for c in range(NCORES)]
    for k, v in pb[0].items():
        m[k] = np.ascontiguousarray(np.stack([p[k] for p in pb], 0)) if k in BC_STACK else v
    res = run_bass_kernel_spmd(nc, [m for _ in range(NCORES)], core_ids=list(range(NCORES)))
    r0 = {k: np.asarray(v) for k, v in res.results[0].items()}
    f32 = np.float32
    r = [{k: r0[k][c] for k in ("y_out", "sd_p", "sq_p", "sd_s", "sq_s", "sf_out", "sf_old", "kv_out")} for c in range(NCORES)]
    y_p = np.concatenate([r[c]["y_out"][0:NOWN] for c in range(8)], 0)[None].astype(f32)
    y_s = np.concatenate([r[c]["y_out"][NOWN + 2:NOWN + 6] for c in range(8)], 0)[:, None].astype(f32)
    sd_p = np.stack([r[h]["sd_p"] for h in range(8)], 0)[None, None].astype(f32)
    sq_p = np.zeros((1, 1, 3, 3072), f32)
    sq_s = np.zeros((1, NS, 3, 3072), f32)
    for h in range(8):
        for s_ in range(3):
            sq_p[0, 0, :, s_ * 1024 + h * 128:s_ * 1024 + (h + 1) * 128] = r[h]["sq_p"][:, s_ * 128:(s_ + 1) * 128]
            sq_s[0, :, :, s_ * 1024 + h * 128:s_ * 1024 + (h + 1) * 128] = r[h]["sq_s"][:, :, s_ * 128:(s_ + 1) * 128]
    sd_s = np.stack([r[h]["sd_s"] for h in range(8)], 1)[None].astype(f32)
    sf_p = r[7]["sf_out"][:, 0:2][:, None].astype(f32)
    kvl = r[7]["kv_out"][0:NOWN]
    kv128_p = kvl[NOWN - 128:, :, 0:2][None].astype(f32)
    kv512_p = kvl[NOWN - 512:, :, 2:4][None].astype(f32)
    kv2048_p = kvl[NOWN - 2048:, :, 4:6][None].astype(f32)
    sf_s = np.zeros((2, NS, 2, 2 * DFF), f32)
    kvs = np.zeros((NS, 2, 6, 128), f32)
    for c in range(8):
        sf_s[:, 4 * c:4 * c + 4, 0] = r[c]["sf_old"]
        sf_s[:, 4 * c:4 * c + 4, 1] = r[c]["sf_out"][:, 2:6]
        kvs[4 * c:4 * c + 4] = r[c]["kv_out"][NOWN + 2:NOWN + 6]
    kv128_s = np.ascontiguousarray(kvs[:, None, :, 0:2])
    kv512_s = np.ascontiguousarray(kvs[:, None, :, 2:4])
    kv2048_s = np.ascontiguousarray(kvs[:, None, :, 4:6])
    return (y_p, y_s, sd_p, sq_p, sf_p, np.ascontiguousarray(kv128_p), np.ascontiguousarray(kv512_p),
            np.ascontiguousarray(kv2048_p), sd_s, sq_s, sf_s, kv128_s, kv512_s, kv2048_s)
```

```python
import numpy as np
import ml_dtypes
from contextlib import ExitStack
import concourse.bass as bass
import concourse.mybir as mybir
from concourse.bass_utils import run_bass_kernel_spmd

F32 = mybir.dt.float32
BF16 = mybir.dt.bfloat16
AF = mybir.ActivationFunctionType
ALU = mybir.AluOpType

NCORES = 8
T_P = 16384
NS = 32
D = 1024
HD = 128
EPS = 1e-6
NDS = 40
SAME_ENGINE_WAITS = True


class Buf:
    __slots__ = ("name", "w", "r", "excl")

    def __init__(self, name=""):
        self.name = name
        self.w = None
        self.r = {}
        self.excl = False


class T:
    def __init__(self, ap, buf=None):
        self.ap = ap
        self.buf = buf if buf is not None else Buf()

    def __getitem__(self, idx):
        return T(self.ap[idx], self.buf)

    def re(self, pat, **kw):
        return T(self.ap.rearrange(pat, **kw), self.buf)

    def sub(self, idx):
        return T(self.ap[idx], Buf())


def A(x):
    return x.ap if isinstance(x, T) else x


def B(*xs):
    return [x.buf for x in xs if isinstance(x, T)]


class Sched:
    ENG = ("pe", "act", "dve", "pool", "sp")

    def __init__(self, nc, es):
        self.nc = nc
        self.hd = {}
        self.cnt = {}
        self.known = {}
        self.ops = {}
        for e in self.ENG:
            self.hd[e] = es.enter_context(nc.semaphore("s_" + e))
            self.cnt[e] = 0
            self.known[e] = {}
            self.ops[e] = []
        for i in range(NDS):
            self.hd[("d", i)] = es.enter_context(nc.semaphore("d%d" % i))
        self.hd["cc"] = es.enter_context(nc.semaphore("cc"))
        self.tiny = T(es.enter_context(nc.sbuf_tensor("tiny", [1, 4], F32))[:])
        self.dval = [0] * NDS
        self.dnext = 0
        self.nops = 0

    def _deps(self, eng, rd, wr):
        need = {}

        def add(k, v):
            if need.get(k, 0) < v:
                need[k] = v

        for b in rd:
            if b.w is not None:
                add(*b.w)
            if b.excl:
                for k, v in b.r.items():
                    if k != eng:
                        add(k, v)
        for b in wr:
            if b.w is not None:
                add(*b.w)
            for k, v in b.r.items():
                add(k, v)
        waits = []
        kn = self.known[eng]
        for k, v in need.items():
            if k == eng and (eng == "pe" or not SAME_ENGINE_WAITS):
                continue
            if kn.get(k, 0) >= v:
                continue
            kn[k] = v
            waits.append((k, v))
        return waits

    def _mark(self, tok, rd, wr):
        k, v = tok
        for b in rd:
            if b.r.get(k, 0) < v:
                b.r[k] = v
        for b in wr:
            b.w = tok
            b.r = {}

    def op(self, eng, fn, rd=(), wr=()):
        waits = self._deps(eng, rd, wr)
        self.cnt[eng] += 1
        tok = (eng, self.cnt[eng])
        self.ops[eng].append((waits, fn, eng, 1))
        self._mark(tok, rd, wr)
        self.nops += 1

    def dma(self, eng, out, in_, **kw):
        rd, wr = B(in_), B(out)
        i = self.dnext
        self.dnext = (self.dnext + 1) % NDS
        waits = self._deps(eng, rd, wr)
        k = ("d", i)
        if self.dval[i] > 0 and self.known[eng].get(k, 0) < self.dval[i]:
            waits.append((k, self.dval[i]))
            self.known[eng][k] = self.dval[i]
        self.dval[i] += 16
        tok = (k, self.dval[i])
        o, s = A(out), A(in_)
        self.ops[eng].append((waits, lambda e: e.dma_start(out=o, in_=s, **kw), k, 16))
        self._mark(tok, rd, wr)
        self.nops += 1

    def flush(self, final=False):
        nc = self.nc
        if final:
            waits = []
            for i in range(NDS):
                if self.dval[i] > 0:
                    waits.append((("d", i), self.dval[i]))
            self.ops["sp"].append((waits, None, None, 0))
        for e in self.ENG:
            waits = []
            kn = self.known[e]
            for k in self.ENG:
                if k != e and kn.get(k, 0) < self.cnt[k]:
                    waits.append((k, self.cnt[k]))
                    kn[k] = self.cnt[k]
            for i in range(NDS):
                k = ("d", i)
                if kn.get(k, 0) < self.dval[i]:
                    waits.append((k, self.dval[i]))
                    kn[k] = self.dval[i]
            if waits:
                self.ops[e].append((waits, None, None, 0))
        hd = self.hd

        def mk(name):
            ops = self.ops[name]

            def body(e):
                self._cur_pid = None
                for waits, fn, k, inc in ops:
                    for wk, wv in waits:
                        e.wait_ge(hd[wk], wv)
                    if fn is not None:
                        fn(e).then_inc(hd[k], inc)

            return body

        with nc.Block() as block:
            block.tensor(mk("pe"))
            block.scalar(mk("act"))
            block.vector(mk("dve"))
            block.gpsimd(mk("pool"))
            block.sync(mk("sp"))
        for e in self.ENG:
            self.ops[e] = []

    def coll_issue(self, ins, outs):
        waits = self._deps("pool", B(ins), B(outs))
        self.ccn = getattr(self, "ccn", 0) + 1
        i_, o_ = A(ins), A(outs)
        self.ops["pool"].append((waits, lambda e: e.collective_compute(
            "AllGather", ALU.bypass, replica_groups=[list(range(NCORES))], ins=[i_], outs=[o_]), "cc", 1))
        return (self.ccn, B(ins), B(outs))

    def coll_relay(self, h):
        n, rd, wr = h
        self.cnt["pool"] += 1
        t = self.tiny.ap
        self.ops["pool"].append(([("cc", n)], lambda e: e.memset(t, 0.0), "pool", 1))
        self._mark(("pool", self.cnt["pool"]), rd, wr)

    def coll(self, ins, outs):
        self.coll_relay(self.coll_issue(ins, outs))

    def _get_pid(self, e):
        if self._cur_pid is None:
            self._cur_pid = e.partition_id()
        return self._cur_pid

    def dma_dyn(self, eng, out, src_fn, rd=()):
        wr = B(out)
        i = self.dnext
        self.dnext = (self.dnext + 1) % NDS
        waits = self._deps(eng, list(rd), wr)
        k = ("d", i)
        if self.dval[i] > 0 and self.known[eng].get(k, 0) < self.dval[i]:
            waits.append((k, self.dval[i]))
            self.known[eng][k] = self.dval[i]
        self.dval[i] += 16
        o = A(out)
        self.ops[eng].append((waits, lambda e: e.dma_start(out=o, in_=src_fn(self._get_pid(e))), k, 16))
        self._mark((k, self.dval[i]), list(rd), wr)
        self.nops += 1

    def mm(self, out, lhsT, rhs, start=True, stop=True):
        o, l, r = A(out), A(lhsT), A(rhs)
        self.op("pe", lambda e: e.matmul(o, lhsT=l, rhs=r, start=start, stop=stop), B(lhsT, rhs), B(out))

    def tr(self, out, in_, ident):
        o, i, d = A(out), A(in_), A(ident)
        self.op("pe", lambda e: e.transpose(o, i, d), B(in_, ident), B(out))

    def act(self, out, in_, func, bias=None, scale=None, accum=None):
        o, i = A(out), A(in_)
        kw = {}
        if bias is not None:
            kw["bias"] = A(bias)
        if scale is not None:
            kw["scale"] = A(scale)
        if accum is not None:
            kw["accum_out"] = A(accum)
        self.op("act", lambda e: e.activation(o, i, func, **kw), B(in_, bias, scale), B(out, accum))

    def ts(self, eng, out, in0, s1, s2, op0, op1=None):
        o, i, a1, a2 = A(out), A(in0), A(s1), A(s2)
        if op1 is None:
            f = lambda e: e.tensor_scalar(o, i, a1, None, op0)
        else:
            f = lambda e: e.tensor_scalar(o, i, a1, a2, op0, op1)
        self.op(eng, f, B(in0, s1, s2), B(out))

    def tt(self, eng, out, in0, in1, op):
        o, i0, i1 = A(out), A(in0), A(in1)
        self.op(eng, lambda e: e.tensor_tensor(o, i0, i1, op), B(in0, in1), B(out))

    def stt(self, eng, out, in0, scalar, in1, op0, op1):
        o, i0, s, i1 = A(out), A(in0), A(scalar), A(in1)
        self.op(eng, lambda e: e.scalar_tensor_tensor(o, i0, s, i1, op0, op1), B(in0, scalar, in1), B(out))

    def cp(self, eng, out, in_):
        o, i = A(out), A(in_)
        if eng == "act":
            self.op("act", lambda e: e.copy(o, i), B(in_), B(out))
        else:
            self.op(eng, lambda e: e.tensor_copy(o, i), B(in_), B(out))

    def recip(self, out, in_):
        o, i = A(out), A(in_)
        self.op("dve", lambda e: e.reciprocal(o, i), B(in_), B(out))

    def memset(self, eng, out, val):
        o = A(out)
        self.op(eng, lambda e: e.memset(o, val), (), B(out))


class Ctx:
    N = [0]

    def __init__(self, nc, es):
        self.nc = nc
        self.es = es

    @property
    def n(self):
        Ctx.N[0] += 1
        return Ctx.N[0]

    def sb(self, shape, dt=F32, name=None):
        t = self.es.enter_context(self.nc.sbuf_tensor("%s_%d" % (name or "sb", self.n), list(shape), dt))
        return T(t[:] if not hasattr(t, "ap") else t.ap())

    def ps(self, shape, dt=F32, name=None):
        t = self.es.enter_context(self.nc.psum_tensor("%s_%d" % (name or "ps", self.n), list(shape), dt))
        r = T(t[:] if not hasattr(t, "ap") else t.ap())
        r.buf.excl = True
        return r

    def dram(self, name, shape, dt, kind):
        return T(self.nc.dram_tensor(name, list(shape), dt, kind=kind).ap())


NBLK_A = T_P // 512
ROWS_A = 4 + T_P + NS


def phase_a(nc, S, io, bo, gath, nblk=NBLK_A, do_samples=True, use_coll=True):
    es = ExitStack()
    with es:
        C = Ctx(nc, es)
        ident = C.sb([128, 128], F32, "ident")
        identb = C.sb([128, 128], BF16, "identb")
        U = C.sb([128, 128], F32, "U")
        Ms = C.sb([128, 128], F32, "Ms")
        ones = C.sb([128, 128], F32, "ones")
        S.dma("sp", ident, io["ident"])
        S.dma("sp", U, io["U"])
        S.dma("sp", Ms, io["Ms"])
        S.memset("dve", ones, 1.0)
        S.cp("dve", identb, ident)
        gmix = C.sb([128, 8], F32, "gmix")
        S.dma("sp", gmix, io["gmix0"])
        wstage = C.sb([128, 8, 514], F32, "wstage")
        S.dma("sp", wstage[:, :, 0:384], io["wqkv"].re("(k p) f -> p k f", p=128))
        S.dma("sp", wstage[:, :, 384:514], io["wzab"].re("(k p) f -> p k f", p=128))
        W = C.sb([128, 8, 514], BF16, "W")
        for k in range(8):
            S.ts("dve", W[:, k, :], wstage[:, k, :], gmix[:, k:k + 1], None, ALU.mult)
        wc = C.sb([128, 3, 4], F32, "wc")
        S.dma("sp", wc, io["wconv"])
        gout = C.sb([128, 128], F32, "gout")
        S.dma("sp", gout, io["gout"])
        alog = C.sb([128, 1], F32, "alog")
        dtb = C.sb([128, 1], F32, "dtb")
        S.dma("sp", alog, io["alog"])
        S.dma("sp", dtb, io["dtb"])
        negA = C.sb([128, 1], F32, "negA")
        S.act(negA, alog, AF.Exp)
        S.ts("dve", negA, negA, -1.0, None, ALU.mult)
        pend = [None]

        xt = [C.sb([128, 1024], F32, "xt") for _ in range(2)]
        junk = C.sb([128, 1024], BF16, "junk")
        ss = [C.sb([128, 2], F32, "ss") for _ in range(2)]
        xn = [C.sb([128, 1024], BF16, "xn") for _ in range(2)]
        xnT = [C.sb([128, 8, 512], BF16, "xnT") for _ in range(2)]
        pre = C.sb([128, 3, 515], F32, "pre")
        S.memset("dve", pre, 0.0)
        acc = C.sb([128, 3, 512], F32, "acc")
        post = [C.sb([128, 3, 512], F32, "post") for _ in range(2)]
        sq = C.sb([128, 2, 512], F32, "sq")
        rinv = C.sb([128, 512], F32, "rinv")
        zab = [[C.sb([128, 130], F32, "zab") for _ in range(4)] for _ in range(2)]
        Sst = [C.sb([128, 128], F32, "Sst") for _ in range(2)]
        S.memset("dve", Sst[0], 0.0)
        tok3 = C.sb([32, 384], F32, "tok3")

        pT = C.ps([128, 1024], BF16, "pT")
        pq = C.ps([128, 512], F32, "pq")
        pz = C.ps([128, 512], F32, "pz")
        po = C.ps([128, 512], F32, "po")
        pn = pq
        pslots = [C.ps([128, 512], F32, "pslot") for _ in range(4)]
        slot_i = [0]

        def slot():
            s = pslots[slot_i[0] % len(pslots)]
            slot_i[0] += 1
            return s[:, 0:128]

        NSET = 2
        cb = []
        for _ in range(NSET):
            d = {}
            for nm in ("gB", "d1", "d2", "E", "EMs", "E2", "ETM", "gamrow", "Kbg", "Vb", "Kd", "L", "M", "L2", "M2",
                       "Y", "QKDt", "QgT", "U0", "WkT", "u", "sz", "gz", "jk", "jk2"):
                d[nm] = C.sb([128, 128], F32, nm)
            d["sc"] = C.sb([128, 12], F32, "sc")
            d["gend"] = C.sb([128, 1], F32, "gend")
            d["og"] = C.sb([128, 128], BF16, "og")
            cb.append(d)
        chunk_no = [0]
        MUL, ADD, SUB = ALU.mult, ALU.add, ALU.subtract

        def chunk_prep(QT, KT, VT, zb, Cn, nlev):
            b = cb[chunk_no[0] % NSET]
            chunk_no[0] += 1
            c = slice(0, Cn)
            sc = b["sc"]
            g, beta, gam, kds, bg, t0, t1, rso0, rso = (sc[c, i:i + 1] for i in range(9))
            gcs = sc[c, 9:11]
            a_l, b_l, z = zb[c, 128:129], zb[c, 129:130], zb[c, 0:128]
            S.act(t0, a_l, AF.Exp, bias=dtb[c, :])
            S.act(t1, t0, AF.Ln, bias=1.0)
            S.tt("dve", g, t1, negA[c, :], MUL)
            S.act(beta, b_l, AF.Sigmoid)
            pg = slot()
            S.mm(pg[c, 0:1], U[c, c], g)
            S.mm(pg[c, 1:2], ones[c, c], g)
            S.cp("dve", gcs, pg[c, 0:2])
            S.ts("dve", b["gB"][c, :], ones[c, :], g, None, MUL)
            prow = slot()
            S.mm(prow[:, c], b["gB"][c, :], U[c, c])
            S.ts("dve", b["d1"][c, c], prow[c, c], gcs[:, 0:1], 0.0, SUB, ALU.max)
            S.act(b["E"][c, c], b["d1"][c, c], AF.Exp, scale=-1.0)
            S.tt("pool", b["EMs"][c, c], b["E"][c, c], Ms[c, c], MUL)
            S.ts("dve", b["d2"][c, c], prow[c, c], gcs[:, 0:1], 0.0, SUB, ALU.min)
            S.act(b["E2"][c, c], b["d2"][c, c], AF.Exp)
            S.tt("pool", b["ETM"][c, c], b["E2"][c, c], U[c, c], MUL)
            S.act(b["gamrow"][:, c], prow[:, c], AF.Exp)
            S.act(b["gend"], prow[:, Cn - 1:Cn], AF.Exp)
            S.act(gam, gcs[:, 0:1], AF.Exp)
            S.act(kds, gcs[:, 0:1], AF.Exp, bias=gcs[:, 1:2], scale=-1.0)
            S.tt("dve", bg, beta, gam, MUL)
            pt = slot()
            S.tr(pt[c, :], KT, ident)
            pv = slot()
            S.tr(pv[c, :], VT, ident)
            S.ts("dve", b["Kbg"][c, :], pt[c, :], bg, None, MUL)
            S.ts("dve", b["Kd"][c, :], pt[c, :], kds, None, MUL)
            S.ts("dve", b["Vb"][c, :], pv[c, :], beta, None, MUL)
            pk = slot()
            S.mm(pk[c, c], KT, KT)
            S.stt("dve", b["L"][c, c], pk[c, c], beta, b["EMs"][c, c], MUL, MUL)
            pqk = slot()
            S.mm(pqk[c, c], KT, QT)
            S.tt("dve", b["QKDt"][c, c], pqk[c, c], b["ETM"][c, c], MUL)
            S.tt("pool", b["QgT"][:, c], QT, b["gamrow"][:, c], MUL)
            pM = slot()
            S.tr(pM[c, c], b["L"][c, c], ident[c, c])
            S.cp("dve", b["M"][c, c], pM[c, c])
            S.stt("dve", b["Y"][c, c], pM[c, c], -1.0, ident[c, c], MUL, ADD)
            Lp, Mp = b["L"], b["M"]
            alt = [(b["L2"], b["M2"]), (b["L"], b["M"])]
            for lv in range(nlev):
                Ln_, Mn_ = alt[lv % 2]
                pM2 = slot()
                S.mm(pM2[c, c], Lp[c, c], Mp[c, c])
                pL2 = slot()
                S.mm(pL2[c, c], Mp[c, c], Lp[c, c])
                S.cp("dve", Mn_[c, c], pM2[c, c])
                S.cp("dve", Ln_[c, c], pL2[c, c])
                pY = slot()
                S.mm(pY[c, c], Ln_[c, c], b["Y"][c, c])
                S.tt("dve", b["Y"][c, c], pY[c, c], b["Y"][c, c], ADD)
                Lp, Mp = Ln_, Mn_
            pU = slot()
            S.mm(pU[c, :], b["Y"][c, c], b["Vb"][c, :])
            S.cp("dve", b["U0"][c, :], pU[c, :])
            pW = slot()
            S.mm(pW[:, c], b["Kbg"][c, :], b["Y"][c, c])
            S.cp("dve", b["WkT"][:, c], pW[:, c])
            return (b, c, z, rso0, rso)

        def chunk_scan(ctx, S_in, S_out, og_dst):
            b, c, z, rso0, rso = ctx
            pu = slot()
            S.mm(pu[c, :], b["WkT"][:, c], S_in)
            S.stt("dve", b["u"][c, :], pu[c, :], -1.0, b["U0"][c, :], MUL, ADD)
            S.mm(po[c, 0:128], b["QgT"][:, c], S_in, start=True, stop=False)
            S.mm(po[c, 0:128], b["QKDt"][c, c], b["u"][c, :], start=False, stop=True)
            pS = slot()
            S.mm(pS, b["Kd"][c, :], b["u"][c, :])
            S.ts("pool", b["jk2"], S_in, b["gend"][:, 0:1], None, MUL)
            S.tt("dve", S_out, pS, b["jk2"], ADD)
            S.act(b["jk"][c, :], po[c, 0:128], AF.Square, accum=rso0)
            S.act(rso, rso0, AF.Sqrt, bias=EPS, scale=1.0 / 128)
            S.recip(rso, rso)
            S.act(b["sz"][c, :], z, AF.Silu)
            S.tt("pool", b["gz"][c, :], b["sz"][c, :], gout[c, :], MUL)
            S.stt("dve", b["og"][c, :], po[c, 0:128], rso, b["gz"][c, :], MUL, MUL)
            S.dma("sp", og_dst, b["og"][c, :])

        def conv_norm(prebuf, accb, postb, n):
            for s in range(3):
                S.ts("dve", accb[:, s, 0:n], prebuf[:, s, 0:n], wc[:, s, 0:1], None, MUL)
                for j in range(1, 4):
                    S.stt("dve", accb[:, s, 0:n], prebuf[:, s, j:j + n], wc[:, s, j:j + 1], accb[:, s, 0:n], MUL, ADD)
            for s in range(3):
                S.act(postb[:, s, 0:n], accb[:, s, 0:n], AF.Silu)
            for s in range(2):
                S.tt("pool", sq[:, s, 0:n], postb[:, s, 0:n], postb[:, s, 0:n], MUL)
            for s in range(2):
                S.mm(pn[:, 0:n], ones, sq[:, s, 0:n])
                S.act(rinv[:, 0:n], pn[:, 0:n], AF.Sqrt, bias=EPS)
                S.recip(rinv[:, 0:n], rinv[:, 0:n])
                if s == 0:
                    S.stt("dve", postb[:, 0, 0:n], postb[:, 0, 0:n], HD ** -0.5, rinv[:, 0:n], MUL, MUL)
                else:
                    S.tt("dve", postb[:, 1, 0:n], postb[:, 1, 0:n], rinv[:, 0:n], MUL)

        def norm_T(xb, rows, ssb, xnb, dstT, col0):
            r = slice(0, rows)
            S.act(junk[r, :], xb[r, :], AF.Square, accum=ssb[r, 0:1])
            S.act(ssb[r, 1:2], ssb[r, 0:1], AF.Sqrt, bias=EPS, scale=1.0 / D)
            S.recip(ssb[r, 1:2], ssb[r, 1:2])
            S.ts("dve", xnb[r, :], xb[r, :], ssb[r, 1:2], None, MUL)
            for k in range(8):
                S.tr(pT[:, k * rows:(k + 1) * rows], xnb[r, k * 128:(k + 1) * 128], identb[r, r])
            S.cp("act", dstT[:, :, col0:col0 + rows], pT[:, 0:8 * rows].re("p (k t) -> p k t", k=8))

        xp = io["xp"]
        n_chunk = 0
        for blk in range(nblk):
            xnTb = xnT[blk % 2]
            postb = post[blk % 2]
            zabb = zab[blk % 2]
            for t in range(4):
                tile = blk * 4 + t
                S.dma("sp", xt[tile % 2], xp[tile * 128:(tile + 1) * 128, :])
                norm_T(xt[tile % 2], 128, ss[tile % 2], xn[tile % 2], xnTb, t * 128)
            for s in range(3):
                for k in range(8):
                    S.mm(pq, W[:, k, s * 128:(s + 1) * 128], xnTb[:, k, :], start=(k == 0), stop=(k == 7))
                S.cp("act", pre[:, s, 3:515], pq)
            for t in range(4):
                for k in range(8):
                    S.mm(pz[:, 0:130], xnTb[:, k, t * 128:(t + 1) * 128], W[:, k, 384:514], start=(k == 0), stop=(k == 7))
                S.cp("dve", zabb[t], pz[:, 0:130])
            if blk == nblk - 1:
                for k in range(8):
                    S.mm(pz[0:3, 0:384], xnTb[:, k, 509:512], W[:, k, 0:384], start=(k == 0), stop=(k == 7))
                S.cp("dve", tok3[0:3, :], pz[0:3, 0:384])
                S.dma("sp", io["sq_p"], tok3[0:3, :])
            conv_norm(pre, acc, postb, 512)
            S.cp("pool", pre[:, :, 0:3], pre[:, :, 512:515])
            ctxs = [None] * 4
            ctxs[0] = chunk_prep(postb[:, 0, 0:128], postb[:, 1, 0:128], postb[:, 2, 0:128], zabb[0], 128, 6)
            for ci in range(4):
                if ci + 1 < 4:
                    cs = slice((ci + 1) * 128, (ci + 2) * 128)
                    ctxs[ci + 1] = chunk_prep(postb[:, 0, cs], postb[:, 1, cs], postb[:, 2, cs], zabb[ci + 1], 128, 6)
                chunk_scan(ctxs[ci], Sst[n_chunk % 2], Sst[(n_chunk + 1) % 2], bo[blk][ci * 128:(ci + 1) * 128, :])
                n_chunk += 1
            if use_coll:
                if pend[0] is not None:
                    S.coll_relay(pend[0])
                pend[0] = S.coll_issue(bo[blk], gath[blk])
        S.dma("sp", io["sd_p"], Sst[n_chunk % 2])

        if do_samples:
            xs_t = xt[0]
            S.dma("sp", xs_t[0:32, :], io["xs"])
            xnTs = xnT[0]
            norm_T(xs_t, 32, ss[0], xn[0], xnTs, 0)
            sqin = xt[1]
            S.dma("sp", sqin[0:96, 0:384], io["sq_in"])
            pre_s = pre
            pv3 = pre_s[:, :, 3:131].re("p s (b j) -> p s b j", j=4)
            for s in range(3):
                for k in range(8):
                    S.mm(pq[:, 0:32], W[:, k, s * 128:(s + 1) * 128], xnTs[:, k, 0:32], start=(k == 0), stop=(k == 7))
                S.cp("act", pv3[:, s, :, 3], pq[:, 0:32])
                S.tr(pn[:, 0:96], sqin[0:96, s * 128:(s + 1) * 128], ident[0:96, 0:96])
                S.cp("dve", pv3[:, s, :, 0:3], pn[:, 0:96].re("p (b j) -> p b j", j=3))
            for k in range(8):
                S.mm(pz[0:32, 0:384], xnTs[:, k, 0:32], W[:, k, 0:384], start=(k == 0), stop=(k == 7))
            S.cp("dve", tok3[0:32, :], pz[0:32, 0:384])
            S.dma("sp", io["sq_s"][:, 2, :], tok3[0:32, :])
            S.dma("sp", io["sq_s"][:, 0:2, :], io["sq_in"].re("(b j) f -> b j f", j=3)[:, 1:3, :])
            post_s = post[0]
            conv_norm(pre_s, acc, post_s, 128)
            sts = [C.sb([128, 128], F32, "sts") for _ in range(4)]
            zs = [C.sb([1, 130], F32, "zs") for _ in range(2)]
            for bi in range(NS):
                col = 4 * bi + 3
                s_in = sts[(2 * bi) % 4]
                s_out = sts[(2 * bi + 1) % 4]
                S.dma("sp", s_in, io["sd_in"][bi])
                for k in range(8):
                    S.mm(pz[0:1, 0:130], xnTs[:, k, bi:bi + 1], W[:, k, 384:514], start=(k == 0), stop=(k == 7))
                zb = zs[bi % 2]
                S.cp("dve", zb, pz[0:1, 0:130])
                ctx = chunk_prep(post_s[:, 0, col:col + 1], post_s[:, 1, col:col + 1], post_s[:, 2, col:col + 1], zb, 1, 0)
                chunk_scan(ctx, s_in, s_out, bo[NBLK_A][bi:bi + 1, :])
                S.dma("sp", io["sd_s"][bi], s_out)
        if pend[0] is not None:
            S.coll_relay(pend[0])
        if do_samples and use_coll:
            S.coll(bo[NBLK_A], gath[NBLK_A])
        S.flush()


def consts():
    i = np.arange(128)
    ident = np.eye(128, dtype=np.float32)
    U = (i[:, None] <= i[None, :]).astype(np.float32)
    Ms = (i[:, None] > i[None, :]).astype(np.float32)
    return {"ident": ident, "U": U, "Ms": Ms}


A_IN = [("xp", [T_P, D]), ("xs", [NS, D]), ("wqkv", [D, 384]), ("wzab", [D, 130]), ("gmix0", [128, 8]),
        ("wconv", [128, 3, 4]), ("gout", [128, 128]), ("alog", [128, 1]), ("dtb", [128, 1]),
        ("sd_in", [NS, 128, 128]), ("sq_in", [96, 384]), ("ident", [128, 128]), ("U", [128, 128]), ("Ms", [128, 128])]
A_OUT = [("sd_p", [128, 128]), ("sq_p", [3, 384]), ("sd_s", [NS, 128, 128]), ("sq_s", [NS, 3, 384])]


def prep_a(inp, h):
    f = np.ascontiguousarray
    w_in = inp["w_a_in"][0]
    hs = slice(h * 128, (h + 1) * 128)
    cols = np.concatenate([np.arange(s * 1024 + h * 128, s * 1024 + (h + 1) * 128) for s in range(3)])
    m = {}
    m["xp"] = f(inp["x_prompt"][0])
    m["xs"] = f(inp["x_sample"][:, 0])
    m["wqkv"] = f(w_in[:, cols])
    zc = np.concatenate([np.arange(3088 + h * 128, 3088 + (h + 1) * 128), [3072 + h], [3080 + h]])
    m["wzab"] = f(w_in[:, zc])
    m["gmix0"] = f(inp["g_mix_norm"][0].reshape(8, 128).T)
    m["wconv"] = f(inp["w_a_conv"][0][:, cols].reshape(4, 3, 128).transpose(2, 1, 0))
    m["gout"] = f(np.broadcast_to(inp["g_a_out_norm"][0][None, :], (128, 128)))
    m["alog"] = f(np.broadcast_to(inp["a_log"][0, h].reshape(1, 1), (128, 1)))
    m["dtb"] = f(np.broadcast_to(inp["a_dt_bias"][0, h].reshape(1, 1), (128, 1)))
    m["sd_in"] = f(inp["state_delta"][0, :, h])
    m["sq_in"] = f(inp["state_qkv_conv"][0][:, :, cols].reshape(96, 384))
    m.update(consts())
    return m


DFF = 2816
NPAIR = 22
GRP = 4


class Banks:
    def __init__(self, C, n):
        self.b = [C.ps([128, 512], F32, "bank") for _ in range(n)]
        self.i = 0

    def get(self):
        t = self.b[self.i % len(self.b)]
        self.i += 1
        return t


def phase_bc(nc, S, io, NT, og_tile_src, kv_exchange, u_exchange, prepass=True):
    NOWN = NT * 128
    NCOL = NOWN + 16
    CH, CS = NOWN, NOWN + 2
    BLK = [(c, min(c + 512, NOWN)) for c in range(0, NOWN, 512)] + [(NOWN, NCOL)]
    OWNB = BLK[:-1]
    LASTB = BLK[-1]
    MUL, ADD, SUB = ALU.mult, ALU.add, ALU.subtract
    es = ExitStack()
    with es:
        C = Ctx(nc, es)
        PS = Banks(C, 7)
        pbf = C.ps([128, 1024], BF16, "pbf")
        ident = C.sb([128, 128], F32, "ident")
        identb = C.sb([128, 128], BF16, "identb")
        onesb = C.sb([128, 128], BF16, "onesb")
        S.dma("sp", ident, io["ident"])
        S.cp("dve", identb, ident)
        S.memset("dve", onesb, 1.0)
        flag = C.sb([128, 1], F32, "flag")
        S.dma("sp", flag, io["flag"])
        io["flag_sb"] = flag
        hT = C.sb([128, 8, NCOL], F32, "hT")
        stage = [C.sb([128, 2816], F32, "stage") for _ in range(2)]
        wbuf = [C.sb([128, 2048], BF16, "wbuf") for _ in range(2)]
        gains = C.sb([128, 6, 8], F32, "gains")
        S.dma("sp", gains, io["gains"])
        wcnt = [0]

        def load_w(src_ap_T, kc, nf, gain):
            i = wcnt[0] % 2
            wcnt[0] += 1
            st = stage[i][:, 0:kc * nf].re("p (k f) -> p k f", k=kc)
            wb = wbuf[i][:, 0:kc * nf].re("p (k f) -> p k f", k=kc)
            S.dma("sp" if i == 0 else "act", st, src_ap_T)
            if gain is None:
                S.cp("pool", wb, st)
            else:
                for k in range(kc):
                    S.ts("pool" if k % 2 else "dve", wb[:, k, :], st[:, k, :], gain[:, k:k + 1], None, MUL)
            return wb

        def fm_linear(Wd, KC, F, gain, xT, consume, blocks, slab=256):
            Wv = Wd.re("(k p) f -> p k f", p=128)
            for f0 in range(0, F, slab):
                nf = min(slab, F - f0)
                wb = load_w(Wv[:, :, f0:f0 + nf], KC, nf, gain)
                for fc in range(nf // 128):
                    for bi, (c0, c1) in blocks:
                        ps = PS.get()
                        for k in range(KC):
                            S.mm(ps[:, 0:c1 - c0], wb[:, k, fc * 128:(fc + 1) * 128], xT[:, k, c0:c1],
                                 start=(k == 0), stop=(k == KC - 1))
                        consume(f0 // 128 + fc, bi, c0, c1, ps)

        def rmsnorm_fm():
            for (c0, c1) in BLK:
                n = c1 - c0
                S.act(X["sqb"][:, :, 0:n], hT[:, :, c0:c1], AF.Square)
                ps = PS.get()
                for k in range(8):
                    S.mm(ps[:, 0:n], onesb, X["sqb"][:, k, 0:n], start=(k == 0), stop=(k == 7))
                S.act(X["rs"][:, 0:n], ps[:, 0:n], AF.Sqrt, bias=EPS, scale=1.0 / D)
                S.recip(X["rs"][:, 0:n], X["rs"][:, 0:n])
                for k in range(8):
                    S.tt("dve" if k % 2 else "pool", X["hnT"][:, k, c0:c1], hT[:, k, c0:c1], X["rs"][:, 0:n], MUL)

        EB = list(enumerate(BLK))

        X = {}

        def stage_b1():
            xrow = [stage[i][:, 0:1024] for i in range(2)]
            ogrow = [wbuf[i][:, 0:1024].re("p (r d) -> p r d", r=8) for i in range(2)]
            for t in range(NT + 1):
                rows = 128 if t < NT else 16
                c0 = t * 128
                xr = xrow[t % 2]
                S.dma("sp", xr[0:rows, :], io["xh"][c0:c0 + rows, :])
                for half in range(2):
                    ps = PS.get()
                    for k in range(4):
                        kk = half * 4 + k
                        S.tr(ps[:, k * rows:(k + 1) * rows], xr[0:rows, kk * 128:(kk + 1) * 128], ident[0:rows, 0:rows])
                    S.cp("act" if half else "dve", hT[:, half * 4:half * 4 + 4, c0:c0 + rows],
                         ps[:, 0:4 * rows].re("p (k t) -> p k t", k=4))
                og = ogrow[t % 2]
                og_tile_src(S, t, og)
                pbb = pbf
                for r in range(8):
                    S.tr(pbb[:, r * rows:(r + 1) * rows], og[0:rows, r, :], identb[0:rows, 0:rows])
                S.cp("act", X["hnT"][:, :, c0:c0 + rows], pbb[:, 0:8 * rows].re("p (k t) -> p k t", k=8))

            def add_h(fc, bi, c0, c1, ps):
                S.tt("dve", hT[:, fc, c0:c1], ps[:, 0:c1 - c0], hT[:, fc, c0:c1], ADD)

            fm_linear(io["w_a_out"], 8, 1024, None, X["hnT"], add_h, EB)


        def ffn(layer, gidx):
            strow = [stage[g][0:8, 0:DFF] for g in range(2)]
            for g in range(2):
                S.dma("sp", strow[g], io["sf_in"][layer][:, g * DFF:(g + 1) * DFF])
            S.dma("sp", X["wcf"], io["wcf"][layer])
            for g in range(2):
                for f in range(NPAIR):
                    if f % 4 == 0:
                        ps = PS.get()
                    S.tr(ps[:, (f % 4) * 8:(f % 4) * 8 + 8], strow[g][:, f * 128:(f + 1) * 128], ident[0:8, 0:8])
                    if f % 4 == 3 or f == NPAIR - 1:
                        nn = f % 4 + 1
                        S.cp("dve", X["stT"][:, g, f - nn + 1:f + 1, :], ps[:, 0:nn * 8].re("p (f b) -> p f b", b=8))
            rmsnorm_fm()
            Wup = io["w_up"][layer]
            gain = gains[:, gidx, :]
            if layer == 1:
                def save_last(fcg, bi, c0, c1, ps):
                    S.cp("dve", X["usave"][:, fcg % 2, fcg // 2, 0:2], ps[:, 0:2])
                if prepass:
                    fm_linear(Wup, 8, 2 * DFF, gain, X["hnT"], save_last, [(0, (NOWN - 2, NOWN))])
                u_exchange(S, X["usave"], X["uprev"])
                for g in range(2):
                    S.ts("dve", X["uprev"][:, g], X["uprev"][:, g], flag[:, 0:1], None, MUL)
            blocks = [(len(BLK) - 1, LASTB)] + list(enumerate(OWNB))
            for f0 in range(0, NPAIR, GRP):
                npair = min(GRP, NPAIR - f0)
                for fi in range(npair):
                    f = f0 + fi
                    prev_ub = [None, None]

                    def consume(fcg, bi, c0, c1, ps, f=f, fi=fi, prev_ub=prev_ub):
                        g = fcg % 2
                        n = c1 - c0
                        w = X["wcf"][:, f, g, :]
                        if bi == len(BLK) - 1:
                            S.cp("dve", X["ulast"][:, g, :], ps[:, 0:16])
                            st = X["stT"][:, g, f, :].re("p (b j) -> p b j", j=2)
                            S.ts("dve", X["cvs"][:, g, :], st[:, :, 0], w[:, 0:1], None, MUL)
                            S.stt("dve", X["cvs"][:, g, :], st[:, :, 1], w[:, 1:2], X["cvs"][:, g, :], MUL, ADD)
                            S.stt("dve", X["cvs"][:, g, :], X["ulast"][:, g, 2:6], w[:, 2:3], X["cvs"][:, g, :], MUL, ADD)
                            S.cp("pool", X["usave"][:, g, f, 2:6], X["ulast"][:, g, 2:6])
                            if g == 0:
                                S.act(X["sgate"][:, CS:CS + 4], X["cvs"][:, 0, :], AF.Silu)
                            else:
                                S.memset("pool", X["actb"][:, fi, c0:c1], 0.0)
                                S.tt("dve", X["actb"][:, fi, CS:CS + 4], X["sgate"][:, CS:CS + 4], X["cvs"][:, 1, :], MUL)
                            return
                        u = X["ub"][g][bi % 2]
                        S.cp("act", u[:, 2:2 + n], ps[:, 0:n])
                        if bi == 0:
                            if layer == 0:
                                S.cp("pool", u[:, 0:2], X["ulast"][:, g, 0:2])
                            else:
                                S.cp("pool", u[:, 0:2], X["uprev"][:, g, f, :])
                        else:
                            S.cp("pool", u[:, 0:2], prev_ub[g][:, 512:514])
                        prev_ub[g] = u
                        if bi == len(OWNB) - 1:
                            S.cp("pool", X["usave"][:, g, f, 0:2], u[:, n:n + 2])
                        cv = X["cvu"][bi % 2]
                        S.ts("dve", cv[:, 0:n], u[:, 0:n], w[:, 0:1], None, MUL)
                        S.stt("dve", cv[:, 0:n], u[:, 1:n + 1], w[:, 1:2], cv[:, 0:n], MUL, ADD)
                        S.stt("dve", cv[:, 0:n], u[:, 2:n + 2], w[:, 2:3], cv[:, 0:n], MUL, ADD)
                        if g == 0:
                            S.act(X["sgate"][:, c0:c1], cv[:, 0:n], AF.Silu)
                        else:
                            S.tt("pool", X["actb"][:, fi, c0:c1], X["sgate"][:, c0:c1], cv[:, 0:n], MUL)

                    Wv = Wup.re("(k p) f -> p k f", p=128)
                    wb = load_w(Wv[:, :, f * 256:(f + 1) * 256], 8, 256, gain)
                    for fc in range(2):
                        for bi, (c0, c1) in blocks:
                            ps = PS.get()
                            for k in range(8):
                                S.mm(ps[:, 0:c1 - c0], wb[:, k, fc * 128:(fc + 1) * 128], X["hnT"][:, k, c0:c1],
                                     start=(k == 0), stop=(k == 7))
                            consume(fc, bi, c0, c1, ps)
                Wd = io["w_down"][layer]
                for half in range(2):
                    wb = load_w(Wd[f0 * 128:(f0 + npair) * 128, half * 512:(half + 1) * 512].re("(k p) f -> p k f", p=128),
                                npair, 512, None)
                    for fo4 in range(4):
                        fo = half * 4 + fo4
                        for bi, (c0, c1) in EB:
                            ps = PS.get()
                            for k in range(npair):
                                S.mm(ps[:, 0:c1 - c0], wb[:, k, fo4 * 128:(fo4 + 1) * 128], X["actb"][:, k, c0:c1],
                                     start=(k == 0), stop=(k == npair - 1))
                            S.tt("dve", hT[:, fo, c0:c1], ps[:, 0:c1 - c0], hT[:, fo, c0:c1], ADD)
            for g in range(2):
                sfrow = stage[g][0:6, 0:DFF]
                for f in range(NPAIR):
                    if f % 4 == 0:
                        ps = PS.get()
                    S.tr(ps[0:6, (f % 4) * 128:(f % 4 + 1) * 128], X["usave"][:, g, f, :], ident)
                    if f % 4 == 3 or f == NPAIR - 1:
                        nn = f % 4 + 1
                        S.cp("dve", sfrow[:, (f - nn + 1) * 128:(f + 1) * 128], ps[0:6, 0:nn * 128])
                S.dma("sp", io["sf_out"][layer][:, g * DFF:(g + 1) * DFF], sfrow)
            S.dma("sp", io["sf_old"][layer], io["sf_in"][layer].re("(b j) f -> b j f", j=2)[:, 1, :])
            if layer == 1 and io.get("post_ffn1") is not None:
                io["post_ffn1"](S, X["usave"])

        prow = [stage[i][:, 0:256] for i in range(2)]

        def ple(layer, gidx):
            for t in range(NT + 1):
                rows = 128 if t < NT else 16
                c0 = t * 128
                pr = prow[t % 2]
                S.dma("sp", pr[0:rows, :], io["pp"][layer][c0:c0 + rows, :])
                ps = PS.get()
                for k in range(2):
                    S.tr(ps[:, k * rows:(k + 1) * rows], pr[0:rows, k * 128:(k + 1) * 128], ident[0:rows, 0:rows])
                S.cp("dve", X["ppT"][:, :, c0:c0 + rows], ps[:, 0:2 * rows].re("p (k t) -> p k t", k=2))
            rmsnorm_fm()
            Wg = io["w_pg"][layer].re("(k p) f -> p k f", p=128)
            Wp = io["w_pp"][layer].re("(k p) f -> p k f", p=128)
            n = 0
            for f0 in range(0, 1024, 256):
                wg = load_w(Wg[:, :, f0:f0 + 256], 8, 256, gains[:, gidx, :])
                wp = load_w(Wp[:, :, f0:f0 + 256], 2, 256, None)
                for fc in range(2):
                    fo = f0 // 128 + fc
                    for bi, (c0, c1) in EB:
                        m = c1 - c0
                        pg = PS.get()
                        for k in range(8):
                            S.mm(pg[:, 0:m], wg[:, k, fc * 128:(fc + 1) * 128], X["hnT"][:, k, c0:c1], start=(k == 0), stop=(k == 7))
                        pq_ = PS.get()
                        for k in range(2):
                            S.mm(pq_[:, 0:m], wp[:, k, fc * 128:(fc + 1) * 128], X["ppT"][:, k, c0:c1], start=(k == 0), stop=(k == 1))
                        sg = X["sgb"][n % 2]
                        n += 1
                        S.act(sg[:, 0:m], pg[:, 0:m], AF.Sigmoid)
                        S.tt("dve", sg[:, 0:m], pq_[:, 0:m], sg[:, 0:m], MUL)
                        S.tt("pool", hT[:, fo, c0:c1], hT[:, fo, c0:c1], sg[:, 0:m], ADD)


        def alloc_norm(C2):
            X["hnT"] = C2.sb([128, 8, NCOL], BF16, "hnT")
            X["sqb"] = C2.sb([128, 8, 512], BF16, "sqb")
            X["rs"] = C2.sb([128, 512], F32, "rs")

        def alloc_ffn(C2):
            alloc_norm(C2)
            X["actb"] = C2.sb([128, GRP, NCOL], BF16, "actb")
            X["ppT"] = X["actb"][:, 0:2, :]
            X["stT"] = C2.sb([128, 2, NPAIR, 8], F32, "stT")
            X["wcf"] = C2.sb([128, NPAIR, 2, 3], F32, "wcf")
            X["usave"] = C2.sb([128, 2, NPAIR, 6], F32, "usave")
            X["uprev"] = C2.sb([128, 2, NPAIR, 2], F32, "uprev")
            X["ulast"] = C2.sb([128, 2, 16], F32, "ulast")
            X["cvs"] = C2.sb([128, 2, 4], F32, "cvs")
            X["ub"] = [[C2.sb([128, 514], F32, "ub") for _ in range(2)] for _ in range(2)]
            X["sgate"] = C2.sb([128, NCOL], F32, "sgate")
            X["cvu"] = [C2.sb([128, 512], F32, "cvu") for _ in range(2)]
            X["sgb"] = [C2.sb([128, 512], F32, "sgb") for _ in range(2)]

        def tm_project(C2, Wd, gidx, nh_norm, nh_tot, gain_rows, sink):
            F = nh_tot * 128
            Wb = C2.sb([128, 8, F], BF16, "Wb")
            Wv = Wd.re("(k p) f -> p k f", p=128)
            for f0 in range(0, F, 256):
                i = wcnt[0] % 2
                wcnt[0] += 1
                st = stage[i][:, 0:2048].re("p (k f) -> p k f", k=8)
                S.dma("sp" if i == 0 else "act", st, Wv[:, :, f0:f0 + 256])
                for k in range(8):
                    S.ts("pool" if k % 2 else "dve", Wb[:, k, f0:f0 + 256], st[:, k, :], gains[:, gidx, k:k + 1], None, MUL)
            gr = C2.sb([128, nh_norm, 128], F32, "gr")
            S.dma("sp", gr, gain_rows)
            of = [C2.sb([128, nh_tot, 128], F32, "of") for _ in range(2)]
            sqk = C2.sb([128, nh_norm, 128], F32, "sqk")
            red = C2.sb([128, 16], F32, "red")
            cs = [C2.sb([128, 2, nh_norm, 16], F32, "cs") for _ in range(2)]
            tr_ = [C2.sb([128, nh_norm, 16], F32, "tr") for _ in range(4)]
            for t in range(NT + 1):
                rows = 128 if t < NT else 16
                r = slice(0, rows)
                c0 = t * 128
                o = of[t % 2]
                for f0 in range(0, F, 512):
                    ps = PS.get()
                    for k in range(8):
                        S.mm(ps[r, :], X["hnT"][:, k, c0:c0 + rows], Wb[:, k, f0:f0 + 512], start=(k == 0), stop=(k == 7))
                    S.cp("act" if (f0 // 512) % 2 else "dve", o[r].re("p h d -> p (h d)")[:, f0:f0 + 512], ps[r, :])
                kn = o[r, 0:nh_norm, :]
                S.tt("pool", sqk[r], kn, kn, MUL)
                S.op("dve", (lambda e, o_=red[r, 0:nh_norm].ap, i_=sqk[r].ap: e.tensor_reduce(o_, i_, mybir.AxisListType.X, ALU.add)),
                     [sqk.buf], [red.buf])
                S.act(red[r, 0:nh_norm], red[r, 0:nh_norm], AF.Sqrt, bias=EPS, scale=1.0 / HD)
                S.recip(red[r, 0:nh_norm], red[r, 0:nh_norm])
                for h in range(nh_norm):
                    S.stt("dve", o[r, h, :], o[r, h, :], red[r, h:h + 1], gr[r, h, :], MUL, MUL)
                cst = cs[t % 2]
                S.dma("act", cst[r], io["rope"][c0:c0 + rows, :, 0:nh_norm, :])
                x1, x2 = o[r, 0:nh_norm, 0:16], o[r, 0:nh_norm, 16:32]
                co, si = cst[r, 0], cst[r, 1]
                S.tt("pool", tr_[0][r], x1, co, MUL)
                S.tt("pool", tr_[1][r], x2, si, MUL)
                S.tt("pool", tr_[2][r], x2, co, MUL)
                S.tt("pool", tr_[3][r], x1, si, MUL)
                S.tt("dve", x1, tr_[0][r], tr_[1][r], SUB)
                S.tt("dve", x2, tr_[2][r], tr_[3][r], ADD)
                sink(t, rows, c0, o)

        def stage_kv(C2):
            alloc_norm(C2)
            rmsnorm_fm()
            kb = [C2.sb([128, 12, 128], BF16, "kb") for _ in range(2)]
            kvt = [C2.sb([128, 3, 512], BF16, "kvt") for _ in range(2)]

            def sink(t, rows, c0, o):
                r = slice(0, rows)
                S.dma("sp", io["kv_out"][c0:c0 + rows].re("t a h d -> t (a h) d"), o[r])
                if t == NT:
                    S.dma("sp", io["kvs_scr"], o[0:16].re("p h d -> p (h d)"))
                    return
                b = kb[t % 2]
                S.cp("act", b, o)
                kt = kvt[t % 2]
                for h in range(6):
                    S.tr(pbf[:, h * 128:(h + 1) * 128], b[:, h, :], identb)
                for g in range(3):
                    S.cp("dve", kt[:, g, 0:256], pbf[:, g * 256:(g + 1) * 256])
                    S.cp("pool", kt[:, g, 256:512], b[:, 6 + 2 * g:8 + 2 * g, :].re("p h d -> p (h d)"))
                S.dma("sp", io["kvt_own"][:, t].re("g p f -> p g f"), kt)

            tm_project(C2, io["w_kv"], 2, 6, 12, io["gk_rows"], sink)

        def stage_q(C2):
            alloc_norm(C2)
            rmsnorm_fm()
            qb = [C2.sb([128, 12, 128], BF16, "qb") for _ in range(2)]

            def sink(t, rows, c0, o):
                r = slice(0, rows)
                b = qb[t % 2]
                S.cp("act", b[r], o[r])
                S.dma("sp", io["q_scr"][c0:c0 + rows], b[r].re("p h d -> p (h d)"))

            tm_project(C2, io["w_q"], 3, 12, 12, io["gq_rows"], sink)

        SPAN = (1, 4, 16)
        SCALE = HD ** -0.5

        def stage_att(C2):
            NPREV = sum(SPAN)
            kvo = C2.sb([128, 3, NT, 514], BF16, "kvo")
            kvp = C2.sb([128, NPREV, 514], BF16, "kvp")
            masks = C2.sb([128, 16, 256], BF16, "masks")
            S.dma("sp", masks, io["masks"].re("m p f -> p m f"))
            S.memset("dve", kvo, 1.0)
            S.memset("pool", kvp, 1.0)
            for g in range(3):
                src = io["kvt_own"][g].re("t p f -> p t f")
                S.dma("sp", kvo[:, g, :, 0:256], src[:, :, 0:256])
                for h in range(2):
                    S.dma("act", kvo[:, g, :, 256 + h * 129:256 + h * 129 + 128], src[:, :, 256 + h * 128:256 + (h + 1) * 128])
            kv_exchange(S, kvp)
            wo = C2.sb([128, 4, 1024], BF16, "wo")
            for half in range(2):
                st = stage[half][:, 0:2048].re("p (k f) -> p k f", k=4)
                S.dma("sp", st, io["w_o"].re("(k p) f -> p k f", p=128)[:, :, half * 512:(half + 1) * 512])
                S.cp("pool", wo[:, :, half * 512:(half + 1) * 512], st)
            onec = C2.sb([128, 1], BF16, "onec")
            S.memset("dve", onec, 1.0)
            qrow = [C2.sb([128, 1536], BF16, "qrow") for _ in range(2)]
            qT = [C2.sb([128, 12, 128], BF16, "qT") for _ in range(2)]
            pT = [C2.sb([128, 256], BF16, "pT") for _ in range(4)]
            rden = C2.sb([128, 4], F32, "rden")
            ob = C2.sb([128, 4, 128], BF16, "ob")
            oT = C2.sb([128, 4, 128], BF16, "oT")
            accb = [PS.b.pop() for _ in range(4)]
            moff = (0, 2, 5)
            npt = 0
            for q in range(NT):
                qr = qrow[q % 2]
                S.dma("act", qr, io["q_scr"][q * 128:(q + 1) * 128])
                qt = qT[q % 2]
                for hh in range(0, 12, 8):
                    nh = min(8, 12 - hh)
                    for h in range(nh):
                        S.tr(pbf[:, h * 128:(h + 1) * 128], qr[:, (hh + h) * 128:(hh + h + 1) * 128], identb)
                    S.cp("dve", qt[:, hh:hh + nh, :], pbf[:, 0:nh * 128].re("p (h t) -> p h t", h=nh))
                combos = [(g, dl) for g in range(3) for dl in range(SPAN[g], -1, -1)]
                for ci, (g, dl) in enumerate(combos):
                    ti = q - dl
                    if ti >= 0:
                        kt = kvo[:, g, ti, :]
                        pv = 0
                    else:
                        kt = kvp[:, sum(SPAN[:g]) + ti + SPAN[g], :]
                        pv = 8
                    mi = moff[g] + (0 if dl == 0 else (2 if dl == SPAN[g] else 1))
                    if g == 0 and dl == 1:
                        mi = 1
                    for G in range(2):
                        ps = PS.get()
                        S.mm(ps[:, 0:256], kt[:, G * 128:(G + 1) * 128],
                             qt[:, g * 4 + G * 2:g * 4 + G * 2 + 2, :].re("p e t -> p (e t)"))
                        p = pT[npt % 4]
                        npt += 1
                        S.act(p, ps[:, 0:256], AF.Exp, scale=SCALE)
                        S.tt("pool", p, p, masks[:, pv + mi, :], MUL)
                        for E in range(2):
                            S.mm(accb[G * 2 + E][:, 0:129], p[:, E * 128:(E + 1) * 128], kt[:, 256 + G * 129:256 + (G + 1) * 129],
                                 start=(ci == 0), stop=(ci == len(combos) - 1))
                for G in range(2):
                    for E in range(2):
                        acc = accb[G * 2 + E][:, 0:129]
                        S.recip(rden[:, G * 2 + E:G * 2 + E + 1], acc[:, 128:129])
                        S.ts("dve", ob[:, G * 2 + E, :], acc[:, 0:128], rden[:, G * 2 + E:G * 2 + E + 1], None, MUL)
                for h in range(4):
                    S.tr(pbf[:, h * 128:(h + 1) * 128], ob[:, h, :], identb)
                S.cp("act", oT, pbf[:, 0:512].re("p (h t) -> p h t", h=4))
                for fo in range(8):
                    ps = PS.get()
                    for k in range(4):
                        S.mm(ps[:, 0:128], wo[:, k, fo * 128:(fo + 1) * 128], oT[:, k, :], start=(k == 0), stop=(k == 3))
                    S.tt("dve", hT[:, fo, q * 128:(q + 1) * 128], ps[:, 0:128], hT[:, fo, q * 128:(q + 1) * 128], ADD)

            PS.b.extend(accb)


        def stage_att_s(C2):
            accb = [PS.b.pop() for _ in range(2)]
            wo = C2.sb([128, 4, 1024], BF16, "wo")
            for half in range(2):
                st = stage[half][:, 0:2048].re("p (k f) -> p k f", k=4)
                S.dma("sp", st, io["w_o"].re("(k p) f -> p k f", p=128)[:, :, half * 512:(half + 1) * 512])
                S.cp("pool", wo[:, :, half * 512:(half + 1) * 512], st)
            qs = C2.sb([16, 1536], BF16, "qs")
            S.dma("sp", qs, io["q_scr"][NOWN:NOWN + 16])
            qsT = C2.sb([128, 12, 16], F32, "qsT")
            qsf = C2.sb([16, 1536], F32, "qsf")
            S.cp("dve", qsf, qs)
            for hh in range(0, 12, 4):
                ps = PS.get()
                for h in range(4):
                    S.tr(ps[:, h * 16:(h + 1) * 16], qsf[:, (hh + h) * 128:(hh + h + 1) * 128], ident[0:16, 0:16])
                S.cp("dve", qsT[:, hh:hh + 4, :], ps[:, 0:64].re("p (h t) -> p h t", h=4))
            oneF = C2.sb([128, 1], F32, "oneF")
            S.memset("dve", oneF, 1.0)
            ct = [C2.sb([128, 512], F32, "ct") for _ in range(2)]
            ktS = [C2.sb([128, 128], F32, "ktS") for _ in range(2)]
            srow = [C2.sb([1, 1536], F32, "srow") for _ in range(2)]
            knT = C2.sb([128, 1], F32, "knT")
            pS = [C2.sb([128, 2], F32, "pS") for _ in range(2)]
            pself = C2.sb([1, 2], F32, "pself")
            osb = C2.sb([2, 130], F32, "osb")
            vS = [C2.sb([128, 129], F32, "vS") for _ in range(2)]
            vself = [C2.sb([1, 129], F32, "vself") for _ in range(2)]
            for i_ in range(2):
                S.memset("dve", vS[i_], 1.0)
                S.memset("dve", vself[i_], 1.0)
            oTs = C2.sb([128, 4, 16], BF16, "oTs")
            S.memset("dve", oTs, 0.0)
            DIL = (1, 4, 16)
            nu = 0
            for b in range(4):
                sr = srow[b % 2]
                S.dma("sp", sr, io["kvs_scr"][2 + b:3 + b, :])
                for G in range(2):
                    acc = accb[G][0:2, 0:129]
                    for g in range(3):
                        c = ct[nu % 2]
                        nu += 1
                        cache = io["cache"][g]
                        Wb_ = cache.ap.shape[1]
                        src = cache[b].re("(m d) a h c -> m d (a h c)", d=DIL[g])[:, 0, :]
                        S.dma("sp" if nu % 2 else "act", c, src)
                        ps = PS.get()
                        S.tr(ps[:, 0:128], c[:, G * 128:(G + 1) * 128], ident)
                        kts = ktS[nu % 2]
                        S.cp("dve", kts, ps[:, 0:128])
                        qcols = qsT[:, g * 4 + G * 2:g * 4 + G * 2 + 2, 2 + b]
                        ps2 = PS.get()
                        S.mm(ps2[:, 0:2], kts, qcols)
                        p = pS[nu % 2]
                        S.act(p, ps2[:, 0:2], AF.Exp, scale=SCALE)
                        vs_ = vS[nu % 2]
                        S.cp("pool", vs_[:, 0:128], c[:, 256 + G * 128:256 + (G + 1) * 128])
                        S.mm(acc, p, vs_, start=(g == 0), stop=False)
                        hk = 2 * g + G
                        ps3 = PS.get()
                        S.mm(ps3[:, 0:1], sr[0:1, hk * 128:(hk + 1) * 128], oneF[0:1, 0:1])
                        S.cp("dve", knT, ps3[:, 0:1])
                        ps4 = PS.get()
                        S.mm(ps4[0:1, 0:2], knT, qcols)
                        S.act(pself, ps4[0:1, 0:2], AF.Exp, scale=SCALE)
                        vf_ = vself[nu % 2]
                        S.cp("pool", vf_[:, 0:128], sr[0:1, (6 + hk) * 128:(7 + hk) * 128])
                        S.mm(acc, pself, vf_, start=False, stop=(g == 2))
                    S.cp("dve", osb[:, 0:129], acc)
                    S.recip(osb[:, 129:130], osb[:, 128:129])
                    S.ts("dve", osb[:, 0:128], osb[:, 0:128], osb[:, 129:130], None, MUL)
                    ps5 = PS.get()
                    S.tr(ps5[:, 0:2], osb[:, 0:128], ident[0:2, 0:2])
                    S.cp("dve", oTs[:, G * 2:G * 2 + 2, 2 + b], ps5[:, 0:2])
            for fo in range(8):
                ps = PS.get()
                for k in range(4):
                    S.mm(ps[:, 0:16], wo[:, k, fo * 128:(fo + 1) * 128], oTs[:, k, :], start=(k == 0), stop=(k == 3))
                S.tt("dve", hT[:, fo, NOWN:NCOL], ps[:, 0:16], hT[:, fo, NOWN:NCOL], ADD)
            PS.b.extend(accb)

        def stage_out():
            yrow = [stage[i][:, 0:1024] for i in range(2)]
            for t in range(NT + 1):
                rows = 128 if t < NT else 16
                c0 = t * 128
                yr = yrow[t % 2]
                for half in range(2):
                    ps = PS.get()
                    for k in range(4):
                        S.tr(ps[0:rows, k * 128:(k + 1) * 128], hT[:, half * 4 + k, c0:c0 + rows], ident)
                    S.cp("act" if half else "dve", yr[0:rows, half * 512:(half + 1) * 512], ps[0:rows, :])
                S.dma("sp", io["y_out"][c0:c0 + rows, :], yr[0:rows, :])

        upto = io.get("upto", 99)
        with ExitStack() as es2:
            alloc_ffn(Ctx(nc, es2))
            stage_b1()
            ffn(0, 0)
            ple(0, 1)
            if io.get("dbg_h") is not None and upto == 1:
                S.dma("sp", io["dbg_h"], hT)
            S.flush()
        if upto >= 2:
            with ExitStack() as es2:
                stage_kv(Ctx(nc, es2))
                S.flush()
            with ExitStack() as es2:
                stage_q(Ctx(nc, es2))
                S.flush()
        if upto >= 3:
            with ExitStack() as es2:
                stage_att(Ctx(nc, es2))
                S.flush()
            with ExitStack() as es2:
                stage_att_s(Ctx(nc, es2))
                if io.get("dbg_h") is not None and upto == 3:
                    S.dma("sp", io["dbg_h"], hT)
                S.flush()
        if upto >= 4:
            with ExitStack() as es2:
                alloc_ffn(Ctx(nc, es2))
                ffn(1, 4)
                ple(1, 5)
                stage_out()
                S.flush()


def ffn_perm():
    idx = []
    for f in range(NPAIR):
        idx += list(range(f * 128, (f + 1) * 128)) + list(range(DFF + f * 128, DFF + (f + 1) * 128))
    return np.array(idx)


def prep_b(inp, c, NT, tok0=None):
    f = np.ascontiguousarray
    NOWN = NT * 128
    NCOL = NOWN + 16
    s = c * NOWN if tok0 is None else tok0
    x = inp["x_prompt"][0]
    m = {}
    xh = np.zeros((NCOL, D), np.float32)
    xh[0:NOWN] = x[s:s + NOWN]
    if s >= 2:
        xh[NOWN:NOWN + 2] = x[s - 2:s]
    xh[NOWN + 2:NOWN + 6] = inp["x_sample"][4 * c:4 * c + 4, 0]
    m["xh"] = xh
    pp = np.zeros((2, NCOL, 256), np.float32)
    pp[:, 0:NOWN] = inp["p_prompt"][:, 0, s:s + NOWN]
    pp[:, NOWN + 2:NOWN + 6] = inp["p_sample"][:, 4 * c:4 * c + 4, 0]
    m["pp"] = pp
    g = [inp["g_ffn_norm"][0], inp["g_ple_norm"][0], inp["g_kv_norm"], inp["g_mix_norm"][1], inp["g_ffn_norm"][1], inp["g_ple_norm"][1]]
    m["gains"] = f(np.stack([a.reshape(8, 128).T for a in g], 1))
    m["w_a_out"] = f(inp["w_a_out"][0])
    perm = ffn_perm()
    m["w_up"] = f(inp["w_ffn_up"][:, :, perm])
    m["w_down"] = f(inp["w_ffn_down"])
    m["w_pg"] = f(inp["w_ple_gate"])
    m["w_pp"] = f(inp["w_ple_proj"])
    wc = inp["w_ffn_conv"]
    m["wcf"] = f(wc.reshape(2, 3, 2, NPAIR, 128).transpose(0, 4, 3, 2, 1))
    m["sf_in"] = f(inp["state_ffn_conv"][:, 4 * c:4 * c + 4].reshape(2, 8, 2 * DFF))
    m["flag"] = np.full((128, 1), 0.0 if s == 0 else 1.0, np.float32)
    m["ident"] = np.eye(128, dtype=np.float32)
    return m


B_IN = lambda NT: [("xh", [NT * 128 + 16, D]), ("pp", [2, NT * 128 + 16, 256]), ("gains", [128, 6, 8]), ("w_a_out", [D, D]),
                   ("w_up", [2, D, 2 * DFF]), ("w_down", [2, DFF, D]), ("w_pg", [2, D, D]), ("w_pp", [2, 256, D]),
                   ("wcf", [2, 128, NPAIR, 2, 3]), ("sf_in", [2, 8, 2 * DFF]), ("flag", [128, 1]), ("ident", [128, 128])]
B_OUT = lambda NT: [("sf_out", [2, 6, 2 * DFF]), ("sf_old", [2, 4, 2 * DFF])]


GROUPS = ((128, 1), (512, 4), (2048, 16))


def att_masks(flag):
    j = np.arange(128)[:, None]
    i = np.arange(128)[None, :]
    out = []
    for (W, d) in GROUPS:
        span = W // 128
        for dl in ([0, 1] if span == 1 else [0, 1, span]):
            dist = 128 * dl + i - j
            m = ((dist >= 0) & (dist <= W) & (dist % d == 0)).astype(np.float32)
            out.append(np.concatenate([m, m], 1))
    own = np.stack(out, 0)
    return np.concatenate([own, own * flag], 0).astype(ml_dtypes.bfloat16)


def rope_tab(pos):
    half = 16
    inv = (np.float32(500000.0) ** (-np.arange(half, dtype=np.float32) * np.float32(2.0) / np.float32(32))).astype(np.float32)
    ang = pos.astype(np.float32)[:, None] * inv[None, :]
    t = np.stack([np.cos(ang), np.sin(ang)], 1).astype(np.float32)
    return np.ascontiguousarray(np.broadcast_to(t[:, :, None, :], (len(pos), 2, 12, 16)))


def prep_c(inp, c, NT, tok0=None):
    f = np.ascontiguousarray
    NOWN = NT * 128
    NCOL = NOWN + 16
    s = c * NOWN if tok0 is None else tok0
    m = {}
    pos = np.zeros(NCOL, np.int64)
    pos[0:NOWN] = s + np.arange(NOWN)
    pos[NOWN:NOWN + 2] = [max(s - 2, 0), max(s - 1, 0)]
    pos[NOWN + 2:NOWN + 6] = 16384
    m["rope"] = rope_tab(pos)
    m["gk_rows"] = f(np.broadcast_to(inp["g_k_norm"][None, None, :], (128, 6, 128)))
    m["gq_rows"] = f(np.broadcast_to(inp["g_q_norm"][0][None, None, :], (128, 12, 128)))
    m["w_kv"] = f(inp["w_kv"])
    m["w_q"] = f(inp["w_q"][0])
    m["w_o"] = f(inp["w_o"][0])
    m["masks"] = att_masks(0.0 if s == 0 else 1.0)
    cs = (c % 8)
    for g, nm in enumerate(("cache_kv_w128", "cache_kv_w512", "cache_kv_w2048")):
        m["cache%d" % g] = f(inp[nm][4 * cs:4 * cs + 4])
    return m


C_IN = lambda NT: [("rope", [NT * 128 + 16, 2, 12, 16], F32), ("gk_rows", [128, 6, 128], F32), ("gq_rows", [128, 12, 128], F32),
                   ("w_kv", [D, 1536], F32), ("w_q", [D, 1536], F32), ("w_o", [512, D], F32), ("masks", [16, 128, 256], BF16),
                   ("cache0", [4, 128, 2, 2, 128], F32), ("cache1", [4, 512, 2, 2, 128], F32), ("cache2", [4, 2048, 2, 2, 128], F32)]
C_OUT = lambda NT: [("kv_out", [NT * 128 + 16, 2, 6, 128]), ("y_out", [NT * 128 + 16, D])]
C_SCR = lambda NT: [("kvs_scr", [16, 1536], F32), ("kvt_own", [3, NT, 128, 512], BF16), ("q_scr", [NT * 128 + 16, 1536], BF16)]


NT_FULL = 16
SPAN_ = (1, 4, 16)
NPIECE_KV = sum(SPAN_)

A_STACK = ("wqkv", "wzab", "wconv", "alog", "dtb", "sd_in", "sq_in")
BC_STACK = ("xh", "pp", "sf_in", "flag", "rope", "masks", "cache0", "cache1", "cache2")


def build_program():
    NT = NT_FULL
    NOWN = NT * 128
    NCOL = NOWN + 16
    nc = bass.Bass("TRN2", target_bir_lowering=False)
    es = ExitStack()
    with es:
        S = Sched(nc, es)
        io = {}

        def mk(n, shp, kind, dt=F32):
            return T(nc.dram_tensor(n, list(shp), dt, kind=kind).ap())

        for n, shp in A_IN:
            io[n] = mk(n, ([NCORES] + shp) if n in A_STACK else shp, "ExternalInput")
        for n, shp in A_OUT:
            io[n] = mk(n, [NCORES] + shp, "ExternalOutput")
        for n, shp in B_IN(NT):
            if n not in io:
                io[n] = mk(n, ([NCORES] + shp) if n in BC_STACK else shp, "ExternalInput")
        for n, shp, dt in C_IN(NT):
            io[n] = mk(n, ([NCORES] + shp) if n in BC_STACK else shp, "ExternalInput", dt)
        for n, shp in B_OUT(NT) + C_OUT(NT):
            io[n] = mk(n, [NCORES] + shp, "ExternalOutput")
        kvs_scr = mk("kvs_scr", [16, 1536], "Internal")
        q_scr = mk("q_scr", [NCOL, 1536], "Internal", BF16)
        kvt_all = [mk("kvt_own%d" % c, [3, NT, 128, 512], "Internal", BF16) for c in range(NCORES)]
        uscr = [mk("uscr%d" % c, [128, 128], "Internal") for c in range(NCORES)]
        go_t = nc.dram_tensor("go", [NBLK_A + 1, NCORES * 512, 128], BF16, kind="Internal").ap()

        for h in range(NCORES):
            io_h = dict(io)
            for n in A_STACK:
                io_h[n] = io[n][h]
            for n, _ in A_OUT:
                io_h[n] = io[n][h]
            bo = [T(go_t[p][h * 512:(h + 1) * 512]) for p in range(NBLK_A + 1)]
            phase_a(nc, S, io_h, bo, None, use_coll=False)

        ucont = T(es.enter_context(nc.sbuf_tensor("ucont", [128, 128], F32))[:])
        for c in range(NCORES):
            io_c = dict(io)
            for n in BC_STACK:
                io_c[n] = io[n][c]
            for n, _ in B_OUT(NT) + C_OUT(NT):
                io_c[n] = io[n][c]
            for k in ("w_up", "w_down", "w_pg", "w_pp", "wcf"):
                io_c[k] = [io[k][0], io[k][1]]
            for k in ("pp", "sf_in", "sf_out", "sf_old"):
                io_c[k] = [io_c[k][0], io_c[k][1]]
            io_c["cache"] = [io_c["cache0"], io_c["cache1"], io_c["cache2"]]
            io_c["kvs_scr"], io_c["q_scr"], io_c["kvt_own"] = kvs_scr, q_scr, kvt_all[c]

            def og_src(S, t, og, c=c):
                if t < NT:
                    row0 = (t % 4) * 128
                    S.dma("sp", og, T(go_t[4 * c + t // 4].rearrange("(r n) d -> n r d", r=NCORES)[row0:row0 + 128]))
                    return
                S.memset("dve", og[0:16], 0.0)
                if c > 0:
                    S.dma("sp", og[0:2], T(go_t[4 * c - 1].rearrange("(r n) d -> n r d", r=NCORES)[510:512]))
                S.dma("sp", og[2:6], T(go_t[NBLK_A].rearrange("(r n) d -> n r d", r=NCORES)[4 * c:4 * c + 4]))

            def kv_ex(S, kvp, c=c):
                if c == 0:
                    return
                base = 0
                for g in range(3):
                    src = kvt_all[c - 1][g][NT - SPAN_[g]:NT].re("t p f -> p t f")
                    dst = kvp[:, base:base + SPAN_[g], :]
                    S.dma("sp", dst[:, :, 0:256], src[:, :, 0:256])
                    for h in range(2):
                        S.dma("act", dst[:, :, 256 + h * 129:256 + h * 129 + 128], src[:, :, 256 + h * 128:256 + (h + 1) * 128])
                    base += SPAN_[g]

            def u_ex(S, usave, uprev, c=c):
                if c == 0:
                    S.memset("dve", uprev, 0.0)
                    return
                S.dma("sp", ucont, uscr[c - 1])
                for g in range(2):
                    S.cp("dve", uprev[:, g], ucont[:, g * 2 * NPAIR:(g + 1) * 2 * NPAIR].re("p (f j) -> p f j", j=2))

            def post_ffn1(S, usave, c=c):
                S.memset("dve", ucont, 0.0)
                for g in range(2):
                    S.cp("dve", ucont[:, g * 2 * NPAIR:(g + 1) * 2 * NPAIR].re("p (f j) -> p f j", j=2), usave[:, g, :, 0:2])
                S.dma("sp", uscr[c], ucont)

            io_c["post_ffn1"] = post_ffn1
            io_c["upto"] = 99
            phase_bc(nc, S, io_c, NT, og_src, kv_ex, u_ex, prepass=False)
        S.flush(final=True)
    return nc


def kernel(**inputs):
    inp = {k: np.asarray(v) for k, v in inputs.items()}
    NT = NT_FULL
    NOWN = NT * 128
    nc = build_program()
    pa = [prep_a(inp, h) for h in range(NCORES)]
    m = dict(pa[0])
    for n in A_STACK:
        m[n] = np.ascontiguousarray(np.stack([p[n] for p in pa], 0))
    pb = [dict(prep_b(inp, c, NT), **prep_c(inp, c, NT)) for c in range(NCORES)]
    for k, v in pb[0].items():
        m[k] = np.ascontiguousarray(np.stack([p[k] for p in pb], 0)) if k in BC_STACK else v
    res = run_bass_kernel_spmd(nc, [m for _ in range(NCORES)], core_ids=list(range(NCORES)))
    r0 = {k: np.asarray(v) for k, v in res.results[0].items()}
    f32 = np.float32
    r = [{k: r0[k][c] for k in ("y_out", "sd_p", "sq_p", "sd_s", "sq_s", "sf_out", "sf_old", "kv_out")} for c in range(NCORES)]
    y_p = np.concatenate([r[c]["y_out"][0:NOWN] for c in range(8)], 0)[None].astype(f32)
    y_s = np.concatenate([r[c]["y_out"][NOWN + 2:NOWN + 6] for c in range(8)], 0)[:, None].astype(f32)
    sd_p = np.stack([r[h]["sd_p"] for h in range(8)], 0)[None, None].astype(f32)
    sq_p = np.zeros((1, 1, 3, 3072), f32)
    sq_s = np.zeros((1, NS, 3, 3072), f32)
    for h in range(8):
        for s_ in range(3):
            sq_p[0, 0, :, s_ * 1024 + h * 128:s_ * 1024 + (h + 1) * 128] = r[h]["sq_p"][:, s_ * 128:(s_ + 1) * 128]
            sq_s[0, :, :, s_ * 1024 + h * 128:s_ * 1024 + (h + 1) * 128] = r[h]["sq_s"][:, :, s_ * 128:(s_ + 1) * 128]
    sd_s = np.stack([r[h]["sd_s"] for h in range(8)], 1)[None].astype(f32)
    sf_p = r[7]["sf_out"][:, 0:2][:, None].astype(f32)
    kvl = r[7]["kv_out"][0:NOWN]
    kv128_p = kvl[NOWN - 128:, :, 0:2][None].astype(f32)
    kv512_p = kvl[NOWN - 512:, :, 2:4][None].astype(f32)
    kv2048_p = kvl[NOWN - 2048:, :, 4:6][None].astype(f32)
    sf_s = np.zeros((2, NS, 2, 2 * DFF), f32)
    kvs = np.zeros((NS, 2, 6, 128), f32)
    for c in range(8):
        sf_s[:, 4 * c:4 * c + 4, 0] = r[c]["sf_old"]
        sf_s[:, 4 * c:4 * c + 4, 1] = r[c]["sf_out"][:, 2:6]
        kvs[4 * c:4 * c + 4] = r[c]["kv_out"][NOWN + 2:NOWN + 6]
    kv128_s = np.ascontiguousarray(kvs[:, None, :, 0:2])
    kv512_s = np.ascontiguousarray(kvs[:, None, :, 2:4])
    kv2048_s = np.ascontiguousarray(kvs[:, None, :, 4:6])
    return (y_p, y_s, sd_p, sq_p, sf_p, np.ascontiguousarray(kv128_p), np.ascontiguousarray(kv512_p),
            np.ascontiguousarray(kv2048_p), sd_s, sq_s, sf_s, kv128_s, kv512_s, kv2048_s)
```
